# Optimizing a Trainium2 kernel written in Bass

```python
import math
import jax
import jax.numpy as jnp
from jax import lax
import numpy as np

D_MODEL = 2048
BATCH = 16
SEQ = 256
DEPTH = 4
DEC_BATCH = 4
DEC_SEQ = 1024
PAST_LEN = 256

GRID_W = 64
N_MIXERS = 4
Q_BLOCK = 128
EPS = 1e-6
ROPE_BASE = 10000.0
NEG = -1e30

N_A = (DEPTH + 3) // N_MIXERS
N_B = (DEPTH + 2) // N_MIXERS
N_C = (DEPTH + 1) // N_MIXERS
N_D = DEPTH // N_MIXERS

A_HEADS = 16
A_DH = 64
B_HEADS = 16
B_DK = 128
B_DV = D_MODEL // B_HEADS
B_CHUNK = 32
C_HEADS = 16
C_NOPE = 128
C_ROPE = 64
C_VD = 128
C_QLORA = 512
C_KVLORA = 256
D_HEADS = 32
D_KV_HEADS = 4
D_DH = 64
D_WINDOW = 128
D_FF = -(-8 * D_MODEL // (3 * 256)) * 256

kernel_name = 'hybrid_diffusion_prefix_trunk_step'


def rms_norm(x, g):
    xf = x.astype(jnp.float32)
    y = xf * lax.rsqrt(jnp.mean(xf * xf, axis=-1, keepdims=True) + EPS)
    return (y * g.astype(jnp.float32)).astype(x.dtype)


def modulate(h, shift, scale):
    return h * (1 + scale) + shift


def grid_positions(n_tok):
    rows = n_tok // GRID_W
    row = jnp.repeat(jnp.arange(rows), GRID_W)
    col = jnp.tile(jnp.arange(GRID_W), rows)
    return row, col


def rope_2d(x):
    n_tok, d = x.shape[1], x.shape[-1]
    row, col = grid_positions(n_tok)
    half = d // 2
    nf = half // 2
    inv_freq = ROPE_BASE ** (-jnp.arange(nf, dtype=jnp.float32) / nf)
    bshape = (n_tok,) + (1,) * (x.ndim - 3) + (nf,)

    def rot(xa, pos):
        ang = (pos.astype(jnp.float32)[:, None] * inv_freq[None, :]).reshape(bshape)
        cos, sin = jnp.cos(ang), jnp.sin(ang)
        x1 = xa[..., :nf].astype(jnp.float32)
        x2 = xa[..., nf:].astype(jnp.float32)
        return jnp.concatenate([x1 * cos - x2 * sin, x2 * cos + x1 * sin], axis=-1)

    y = jnp.concatenate([rot(x[..., :half], row), rot(x[..., half:], col)], axis=-1)
    return y.astype(x.dtype)


def map_query_blocks(body, q):
    b, s = q.shape[:2]
    nb = s // Q_BLOCK
    qb = jnp.moveaxis(q.reshape((b, nb, Q_BLOCK) + q.shape[2:]), 1, 0)
    out = lax.map(body, (qb, jnp.arange(nb)))
    return jnp.moveaxis(out, 0, 1).reshape((b, s) + out.shape[3:])


def softmax_with_sink(s, sink):
    m = jnp.maximum(jnp.max(s, axis=-1, keepdims=True), sink)
    e = jnp.exp(s - m)
    return e / (jnp.sum(e, axis=-1, keepdims=True) + jnp.exp(sink - m))


def gqa_core(q, k, v, sink=None):
    h, dh = q.shape[2], q.shape[3]
    g = k.shape[2]
    r = h // g
    scale = dh ** -0.5

    def body(args):
        qb, _ = args
        b, nq = qb.shape[:2]
        qg = qb.reshape(b, nq, g, r, dh)
        s = jnp.einsum('bqgrd,bkgd->bgrqk', qg, k).astype(jnp.float32) * scale
        if sink is None:
            p = jax.nn.softmax(s, axis=-1)
        else:
            p = softmax_with_sink(s, sink.astype(jnp.float32).reshape(g, r, 1, 1))
        o = jnp.einsum('bgrqk,bkge->bqgre', p.astype(v.dtype), v)
        return o.reshape(b, nq, h, o.shape[-1])

    return map_query_blocks(body, q)


def diff_lambda(lam_p, layer_idx):
    lam_init = 0.8 - 0.6 * math.exp(-0.3 * layer_idx)
    lp = lam_p.astype(jnp.float32)
    lam = jnp.exp(jnp.sum(lp[0] * lp[1])) - jnp.exp(jnp.sum(lp[2] * lp[3])) + lam_init
    return lam, lam_init


def diff_attn_project(h, wq, wk, wv):
    b, t, _ = h.shape
    q = (h @ wq).reshape(b, t, A_HEADS, 2, A_DH)
    k = (h @ wk).reshape(b, t, A_HEADS, 2, A_DH)
    v = (h @ wv).reshape(b, t, A_HEADS, 2 * A_DH)
    return q, k, v


def diff_attn_core(q, k, v, lam):
    scale = A_DH ** -0.5

    def body(args):
        qb, _ = args
        s = jnp.einsum('bqhcd,bkhcd->bhcqk', qb, k).astype(jnp.float32) * scale
        pr = jax.nn.softmax(s, axis=-1)
        w = pr[:, :, 0] - lam * pr[:, :, 1]
        return jnp.einsum('bhqk,bkhe->bqhe', w.astype(v.dtype), v)

    return map_query_blocks(body, q)


def diff_attn_output(o, subln, lam_init, wo):
    o = rms_norm(o, subln) * (1.0 - lam_init)
    return o.reshape(o.shape[:2] + (-1,)) @ wo


def diff_attn_context(h, wq, wk, wv, wo, lam_p, subln, layer_idx):
    b, t, _ = h.shape
    q, k, v = diff_attn_project(h, wq, wk, wv)
    lam, lam_init = diff_lambda(lam_p, layer_idx)
    y = diff_attn_output(diff_attn_core(q, k, v, lam), subln, lam_init, wo)
    return y, k.reshape(b, t, A_HEADS, 2 * A_DH), v


def diff_attn_latent(h, k_ctx, v_ctx, wq, wk, wv, wo, lam_p, subln, layer_idx):
    q, k, v = diff_attn_project(h, wq, wk, wv)
    q, k = rope_2d(q), rope_2d(k)
    b, n = k_ctx.shape[:2]
    k_all = jnp.concatenate([k_ctx.reshape(b, n, A_HEADS, 2, A_DH), k], axis=1)
    v_all = jnp.concatenate([v_ctx, v], axis=1)
    lam, lam_init = diff_lambda(lam_p, layer_idx)
    return diff_attn_output(diff_attn_core(q, k_all, v_all, lam), subln, lam_init, wo)


def hgrn2_lower_bound(lower, layer_idx):
    p = jax.nn.softmax(lower.astype(jnp.float32), axis=0)
    return (jnp.cumsum(p, axis=0) - p[0])[layer_idx]


def gla_chunked(q, k, v, logf, s0):
    b, t, h, _ = q.shape
    n = t // B_CHUNK

    def chunks(a):
        return jnp.moveaxis(a.reshape((b, n, B_CHUNK) + a.shape[2:]), 1, 0).astype(jnp.float32)

    qc, kc, vc, gc = chunks(q), chunks(k), chunks(v), chunks(logf)
    causal = jnp.tril(jnp.ones((B_CHUNK, B_CHUNK), dtype=bool))

    def step(s, inp):
        qi, ki, vi, gi = inp
        bcum = jnp.cumsum(gi, axis=1)
        q_dec = qi * jnp.exp(bcum)
        k_inv = ki * jnp.exp(-bcum)
        a = jnp.where(causal, jnp.einsum('bthd,bshd->bhts', q_dec, k_inv), 0.0)
        o = jnp.einsum('bhts,bshv->bthv', a, vi) + jnp.einsum('bthd,bhdv->bthv', q_dec, s)
        b_last = bcum[:, -1]
        k_end = ki * jnp.exp(b_last[:, None] - bcum)
        s_new = jnp.exp(b_last)[..., None] * s + jnp.einsum('bshd,bshv->bhdv', k_end, vi)
        return s_new, o

    s_fin, o = lax.scan(step, s0.astype(jnp.float32), (qc, kc, vc, gc))
    o = jnp.moveaxis(o, 0, 1).reshape(b, t, h, -1)
    return o, s_fin


def hgrn2_mixer(h, s0_fwd, s0_bwd, wq, wi, wf, wg, gnorm, wo, lb):
    b, t, _ = h.shape
    q = jax.nn.silu(h @ wq).reshape(b, t, B_HEADS, B_DK)
    inp = (h @ wi).reshape(b, t, B_HEADS, B_DV)
    outs, states = [], []
    for d, s0 in enumerate((s0_fwd, s0_bwd)):
        z = (h @ wf[d]).astype(jnp.float32)
        f = lb[d] + (1.0 - lb[d]) * jax.nn.sigmoid(z)
        logf = jnp.log(f).reshape(b, t, B_HEADS, B_DK)
        kk = (1.0 - f).reshape(b, t, B_HEADS, B_DK)
        qd, vd = q, inp
        if d == 1:
            qd, kk, vd, logf = (jnp.flip(a, axis=1) for a in (qd, kk, vd, logf))
        o, s = gla_chunked(qd, kk, vd, logf, s0)
        if d == 1:
            o = jnp.flip(o, axis=1)
        outs.append(o)
        states.append(s.astype(h.dtype))
    g = (h @ wg).reshape(b, t, B_HEADS, B_DV).astype(jnp.float32)
    o = rms_norm(outs[0] + outs[1], gnorm) * jax.nn.silu(g)
    y = o.reshape(b, t, B_HEADS * B_DV).astype(h.dtype) @ wo
    return y, states[0], states[1]


def mla_queries(h, wdq, qnorm, wuq):
    b, t, _ = h.shape
    q = (rms_norm(h @ wdq, qnorm) @ wuq).reshape(b, t, C_HEADS, C_NOPE + C_ROPE)
    return q[..., :C_NOPE], q[..., C_NOPE:]


def mla_compress(h, wdkv, kvnorm):
    ckv = h @ wdkv
    return rms_norm(ckv[..., :C_KVLORA], kvnorm), ckv[..., C_KVLORA:]


def mla_expand(ckv, kpe, wuk, wuv):
    b, n, _ = ckv.shape
    k_nope = (ckv @ wuk).reshape(b, n, C_HEADS, C_NOPE)
    v = (ckv @ wuv).reshape(b, n, C_HEADS, C_VD)
    k_pe = jnp.broadcast_to(kpe[:, :, None, :], (b, n, C_HEADS, C_ROPE))
    return jnp.concatenate([k_nope, k_pe], axis=-1), v


def mla_out(o, wo):
    return o.reshape(o.shape[:2] + (-1,)) @ wo


def mla_context(h, wdq, qnorm, wuq, wdkv, kvnorm, wuk, wuv, wo):
    q_nope, q_pe = mla_queries(h, wdq, qnorm, wuq)
    ckv, kpe = mla_compress(h, wdkv, kvnorm)
    k, v = mla_expand(ckv, kpe, wuk, wuv)
    o = gqa_core(jnp.concatenate([q_nope, q_pe], axis=-1), k, v)
    return mla_out(o, wo), ckv, kpe


def mla_latent(h, ckv_ctx, kpe_ctx, wdq, qnorm, wuq, wdkv, kvnorm, wuk, wuv, wo):
    q_nope, q_pe = mla_queries(h, wdq, qnorm, wuq)
    ckv, kpe = mla_compress(h, wdkv, kvnorm)
    q = jnp.concatenate([q_nope, rope_2d(q_pe)], axis=-1)
    ckv_all = jnp.concatenate([ckv_ctx, ckv], axis=1)
    kpe_all = jnp.concatenate([kpe_ctx, rope_2d(kpe)], axis=1)
    k, v = mla_expand(ckv_all, kpe_all, wuk, wuv)
    return mla_out(gqa_core(q, k, v), wo)


def swa_project(h, wq, wk, wv):
    b, t, _ = h.shape
    q = (h @ wq).reshape(b, t, D_HEADS, D_DH)
    k = (h @ wk).reshape(b, t, D_KV_HEADS, D_DH)
    v = (h @ wv).reshape(b, t, D_KV_HEADS, D_DH)
    return q, k, v


def swa_latent_attention(q, k, v, k_ctx, v_ctx, sink):
    b, t, h, dh = q.shape
    g = k.shape[2]
    r = h // g
    band = Q_BLOCK + 2 * D_WINDOW
    pad = ((0, 0), (D_WINDOW, D_WINDOW), (0, 0), (0, 0))
    k_pad = jnp.pad(k, pad)
    v_pad = jnp.pad(v, pad)
    sink_b = sink.astype(jnp.float32).reshape(g, r, 1, 1)
    scale = dh ** -0.5

    def body(args):
        qb, n = args
        start = n * Q_BLOCK
        kb = lax.dynamic_slice_in_dim(k_pad, start, band, axis=1)
        vb = lax.dynamic_slice_in_dim(v_pad, start, band, axis=1)
        qpos = start + jnp.arange(Q_BLOCK)
        kpos = start - D_WINDOW + jnp.arange(band)
        valid = ((jnp.abs(qpos[:, None] - kpos[None, :]) <= D_WINDOW)
                 & (kpos >= 0)[None, :] & (kpos < t)[None, :])
        qg = qb.reshape(b, Q_BLOCK, g, r, dh)
        s_ctx = jnp.einsum('bqgrd,bkgd->bgrqk', qg, k_ctx).astype(jnp.float32)
        s_lat = jnp.einsum('bqgrd,bkgd->bgrqk', qg, kb).astype(jnp.float32)
        s_lat = jnp.where(valid, s_lat, NEG)
        s = jnp.concatenate([s_ctx, s_lat], axis=-1) * scale
        p = softmax_with_sink(s, sink_b)
        vals = jnp.concatenate([v_ctx, vb], axis=1)
        o = jnp.einsum('bgrqk,bkge->bqgre', p.astype(v.dtype), vals)
        return o.reshape(b, Q_BLOCK, h, dh)

    return map_query_blocks(body, q)


def swa_context(h, wq, wk, wv, sink, wo):
    q, k, v = swa_project(h, wq, wk, wv)
    o = gqa_core(q, k, v, sink)
    return o.reshape(o.shape[:2] + (-1,)) @ wo, k, v


def swa_latent(h, k_ctx, v_ctx, wq, wk, wv, sink, wo):
    q, k, v = swa_project(h, wq, wk, wv)
    o = swa_latent_attention(rope_2d(q), rope_2d(k), v, k_ctx, v_ctx, sink)
    return o.reshape(o.shape[:2] + (-1,)) @ wo


def swiglu(h, wg, wu, wd):
    return (jax.nn.silu(h @ wg) * (h @ wu)) @ wd


def setup_inputs(seed: int = 0) -> dict:
    key = jax.random.key(seed)
    ks = iter(jax.random.split(key, 64))
    D = D_MODEL

    def nrm(shape, scale=1.0):
        return jax.random.normal(next(ks), shape, jnp.float32) * scale

    def lin(shape, mult=1.0):
        return nrm(shape, mult * shape[-2] ** -0.5)

    def gain(shape):
        return 1.0 + nrm(shape, 0.1)

    return {
        'x_prompt': nrm((BATCH, SEQ, D)),
        'x_sample': nrm((DEC_BATCH, DEC_SEQ, D)),
        'cache_a_k': nrm((DEC_BATCH, N_A, PAST_LEN, A_HEADS, 2 * A_DH)),
        'cache_a_v': nrm((DEC_BATCH, N_A, PAST_LEN, A_HEADS, 2 * A_DH)),
        'state_b_fwd': nrm((DEC_BATCH, N_B, B_HEADS, B_DK, B_DV), 0.5),
        'state_b_bwd': nrm((DEC_BATCH, N_B, B_HEADS, B_DK, B_DV), 0.5),
        'cache_c_ckv': nrm((DEC_BATCH, N_C, PAST_LEN, C_KVLORA)),
        'cache_c_kpe': nrm((DEC_BATCH, N_C, PAST_LEN, C_ROPE)),
        'cache_d_k': nrm((DEC_BATCH, N_D, PAST_LEN, D_KV_HEADS, D_DH)),
        'cache_d_v': nrm((DEC_BATCH, N_D, PAST_LEN, D_KV_HEADS, D_DH)),
        'c': nrm((DEC_BATCH, D)),
        'c_ctx': nrm((D,)),
        'ada_w': lin((DEPTH, D, 6 * D), 0.5),
        'ada_b': nrm((DEPTH, 6 * D), 0.02),
        'norm_mix_pre': gain((DEPTH, D)),
        'norm_mix_post': gain((DEPTH, D)),
        'norm_ffn_pre': gain((DEPTH, D)),
        'norm_ffn_post': gain((DEPTH, D)),
        'a_wq': lin((N_A, D, A_HEADS * 2 * A_DH)),
        'a_wk': lin((N_A, D, A_HEADS * 2 * A_DH)),
        'a_wv': lin((N_A, D, A_HEADS * 2 * A_DH)),
        'a_wo': lin((N_A, A_HEADS * 2 * A_DH, D)),
        'a_lambda': nrm((N_A, 4, A_DH), 0.1),
        'a_subln': gain((N_A, 2 * A_DH)),
        'b_wq': lin((N_B, D, B_HEADS * B_DK)),
        'b_wi': lin((N_B, D, B_HEADS * B_DV)),
        'b_wf': lin((N_B, 2, D, B_HEADS * B_DK)),
        'b_lower': gain((DEPTH, 2, B_HEADS * B_DK)),
        'b_wg': lin((N_B, D, B_HEADS * B_DV)),
        'b_gnorm': gain((N_B, B_DV)),
        'b_wo': lin((N_B, B_HEADS * B_DV, D)),
        'c_wdq': lin((N_C, D, C_QLORA)),
        'c_qnorm': gain((N_C, C_QLORA)),
        'c_wuq': lin((N_C, C_QLORA, C_HEADS * (C_NOPE + C_ROPE))),
        'c_wdkv': lin((N_C, D, C_KVLORA + C_ROPE)),
        'c_kvnorm': gain((N_C, C_KVLORA)),
        'c_wuk': lin((N_C, C_KVLORA, C_HEADS * C_NOPE)),
        'c_wuv': lin((N_C, C_KVLORA, C_HEADS * C_VD)),
        'c_wo': lin((N_C, C_HEADS * C_VD, D)),
        'd_wq': lin((N_D, D, D_HEADS * D_DH)),
        'd_wk': lin((N_D, D, D_KV_HEADS * D_DH)),
        'd_wv': lin((N_D, D, D_KV_HEADS * D_DH)),
        'd_sink': nrm((N_D, D_HEADS), 0.5),
        'd_wo': lin((N_D, D_HEADS * D_DH, D)),
        'ffn_wg': lin((DEPTH, D, D_FF)),
        'ffn_wu': lin((DEPTH, D, D_FF)),
        'ffn_wd': lin((DEPTH, D_FF, D)),
    }


def reference(x_prompt, x_sample, cache_a_k, cache_a_v, state_b_fwd, state_b_bwd,
              cache_c_ckv, cache_c_kpe, cache_d_k, cache_d_v, c, c_ctx,
              ada_w, ada_b, norm_mix_pre, norm_mix_post, norm_ffn_pre, norm_ffn_post,
              a_wq, a_wk, a_wv, a_wo, a_lambda, a_subln,
              b_wq, b_wi, b_wf, b_lower, b_wg, b_gnorm, b_wo,
              c_wdq, c_qnorm, c_wuq, c_wdkv, c_kvnorm, c_wuk, c_wuv, c_wo,
              d_wq, d_wk, d_wv, d_sink, d_wo,
              ffn_wg, ffn_wu, ffn_wd):
    ctx = x_prompt
    lat = x_sample
    new_a_k, new_a_v = [], []
    new_b_fwd, new_b_bwd = [], []
    new_c_ckv, new_c_kpe = [], []
    new_d_k, new_d_v = [], []
    for i in range(DEPTH):
        m, j = i % N_MIXERS, i // N_MIXERS
        mod_c = jnp.split(jax.nn.silu(c_ctx) @ ada_w[i] + ada_b[i], 6, axis=-1)
        mod_l = [u[:, None, :] for u in jnp.split(jax.nn.silu(c) @ ada_w[i] + ada_b[i], 6, axis=-1)]

        hc = modulate(rms_norm(ctx, norm_mix_pre[i]), mod_c[0], mod_c[1])
        hl = modulate(rms_norm(lat, norm_mix_pre[i]), mod_l[0], mod_l[1])
        if m == 0:
            oc, kc, vc = diff_attn_context(hc, a_wq[j], a_wk[j], a_wv[j], a_wo[j], a_lambda[j], a_subln[j], i)
            ol = diff_attn_latent(hl, cache_a_k[:, j], cache_a_v[:, j], a_wq[j], a_wk[j], a_wv[j],
                                  a_wo[j], a_lambda[j], a_subln[j], i)
            new_a_k.append(kc)
            new_a_v.append(vc)
        elif m == 1:
            lb = hgrn2_lower_bound(b_lower, i)
            zeros = jnp.zeros((ctx.shape[0], B_HEADS, B_DK, B_DV), ctx.dtype)
            oc, sf, sb = hgrn2_mixer(hc, zeros, zeros, b_wq[j], b_wi[j], b_wf[j], b_wg[j],
                                     b_gnorm[j], b_wo[j], lb)
            ol, _, _ = hgrn2_mixer(hl, state_b_fwd[:, j], state_b_bwd[:, j], b_wq[j], b_wi[j], b_wf[j],
                                   b_wg[j], b_gnorm[j], b_wo[j], lb)
            new_b_fwd.append(sf)
            new_b_bwd.append(sb)
        elif m == 2:
            oc, ckv, kpe = mla_context(hc, c_wdq[j], c_qnorm[j], c_wuq[j], c_wdkv[j], c_kvnorm[j],
                                       c_wuk[j], c_wuv[j], c_wo[j])
            ol = mla_latent(hl, cache_c_ckv[:, j], cache_c_kpe[:, j], c_wdq[j], c_qnorm[j], c_wuq[j],
                            c_wdkv[j], c_kvnorm[j], c_wuk[j], c_wuv[j], c_wo[j])
            new_c_ckv.append(ckv)
            new_c_kpe.append(kpe)
        else:
            oc, kc, vc = swa_context(hc, d_wq[j], d_wk[j], d_wv[j], d_sink[j], d_wo[j])
            ol = swa_latent(hl, cache_d_k[:, j], cache_d_v[:, j], d_wq[j], d_wk[j], d_wv[j],
                            d_sink[j], d_wo[j])
            new_d_k.append(kc)
            new_d_v.append(vc)
        ctx = ctx + mod_c[2] * rms_norm(oc, norm_mix_post[i])
        lat = lat + mod_l[2] * rms_norm(ol, norm_mix_post[i])

        hc = modulate(rms_norm(ctx, norm_ffn_pre[i]), mod_c[3], mod_c[4])
        hl = modulate(rms_norm(lat, norm_ffn_pre[i]), mod_l[3], mod_l[4])
        ctx = ctx + mod_c[5] * rms_norm(swiglu(hc, ffn_wg[i], ffn_wu[i], ffn_wd[i]), norm_ffn_post[i])
        lat = lat + mod_l[5] * rms_norm(swiglu(hl, ffn_wg[i], ffn_wu[i], ffn_wd[i]), norm_ffn_post[i])

    return (ctx, lat,
            jnp.stack(new_a_k, axis=1), jnp.stack(new_a_v, axis=1),
            jnp.stack(new_b_fwd, axis=1), jnp.stack(new_b_bwd, axis=1),
            jnp.stack(new_c_ckv, axis=1), jnp.stack(new_c_kpe, axis=1),
            jnp.stack(new_d_k, axis=1), jnp.stack(new_d_v, axis=1))
```

```python
import math
import numpy as np
from contextlib import ExitStack
import concourse.bass as bass
import concourse.mybir as mybir
from concourse.bass_utils import run_bass_kernel_spmd

F32 = mybir.dt.float32
BF16 = mybir.dt.bfloat16
AF = mybir.ActivationFunctionType
ALU = mybir.AluOpType
AX = mybir.AxisListType

EPOCH = 4096
T = 1024
NPAST = 256
NK = 1280
DM = 2048
KC = 16
DFF = 5632
EPS = 1e-6
NEGBIG = -30000.0


class Dep:
    __slots__ = ("w", "r", "excl")

    def __init__(self, excl=False):
        self.w = None
        self.r = []
        self.excl = excl


class Eng:
    def __init__(self, kb, name, h):
        self.kb, self.name, self.h = kb, name, h
        self.sems = []
        self.count = 0
        self.waited = {}
        self.n_inst = 0

    def cur_sem(self):
        if not self.sems or self.count >= EPOCH:
            self.sems.append(self.kb.new_sem(f"{self.name}{len(self.sems)}"))
            self.count = 0
        return self.sems[-1]


class KB:
    def __init__(self, nc, stack):
        self.nc = nc
        self.stack = stack
        self.nsem = 0
        self.dry = False
        self.E = {
            "pe": Eng(self, "pe", nc.tensor),
            "act": Eng(self, "act", nc.scalar),
            "dve": Eng(self, "dve", nc.vector),
            "pool": Eng(self, "pool", nc.gpsimd),
            "sp": Eng(self, "sp", nc.sync),
        }
        self.dma_ring = {}
        self.dma_i = {}
        self.NR = 12

    def new_sem(self, name):
        self.nsem += 1
        return self.stack.enter_context(self.nc.semaphore(f"s_{name}_{self.nsem}"))

    def _wait(self, e, tok):
        if tok is None:
            return
        sem, val, key = tok
        if key[0] == "pe" and e.name == "pe":
            return
        if e.waited.get(key, 0) >= val:
            return
        e.h.wait_ge(sem, val)
        e.n_inst += 1
        e.waited[key] = val

    def _pre(self, e, rd, wr):
        for d in rd:
            self._wait(e, d.w)
            if d.excl:
                for t in d.r:
                    if t[2][0] != e.name:
                        self._wait(e, t)
        for d in wr:
            self._wait(e, d.w)
            for t in d.r:
                self._wait(e, t)

    def _post(self, tok, rd, wr):
        for d in rd:
            d.r.append(tok)
            if len(d.r) > 16:
                best = {}
                for t in d.r:
                    if t[2] not in best or best[t[2]][1] < t[1]:
                        best[t[2]] = t
                d.r = list(best.values())
        for d in wr:
            d.w = tok
            d.r = []

    def op(self, en, fn, rd=(), wr=()):
        if self.dry:
            return None
        e = self.E[en]
        self._pre(e, rd, wr)
        sem = e.cur_sem()
        inst = fn(e.h)
        inst.then_inc(sem, 1)
        e.count += 1
        e.n_inst += 1
        tok = (sem, e.count, (en, len(e.sems)))
        self._post(tok, rd, wr)
        return tok

    def pe_group(self, fns, rd=(), wr=()):
        if self.dry:
            return None
        e = self.E["pe"]
        self._pre(e, rd, wr)
        sem = e.cur_sem()
        inst = None
        for fn in fns:
            inst = fn(e.h)
            e.n_inst += 1
        inst.then_inc(sem, 1)
        e.count += 1
        tok = (sem, e.count, ("pe", len(e.sems)))
        self._post(tok, rd, wr)
        return tok

    def dma(self, qn, out, in_, rd=(), wr=()):
        if self.dry:
            return None
        e = self.E[qn]
        ring = self.dma_ring.setdefault(qn, [])
        i = self.dma_i.get(qn, 0)
        self.dma_i[qn] = i + 1
        j = i % self.NR
        if j >= len(ring):
            ring.append([self.new_sem(f"dma_{qn}{j}"), 0])
        slot = ring[j]
        key = ("dma", qn, j)
        if slot[1] > 0:
            self._wait(e, (slot[0], 16 * slot[1], key))
        self._pre(e, rd, wr)
        inst = e.h.dma_start(out=out, in_=in_)
        slot[1] += 1
        inst.then_inc(slot[0], 16)
        e.n_inst += 1
        tok = (slot[0], 16 * slot[1], key)
        self._post(tok, rd, wr)
        return tok

    def retire(self, olds, news):
        toks = []
        for d in olds:
            if d.w is not None:
                toks.append(d.w)
            toks.extend(d.r)
        for d in news:
            d.r.extend(toks)


IN_SPECS = [
    ("xT", (DM, T)), ("condT", (128, 16)), ("ada_w", (4, DM, 6 * DM)), ("ada_bT", (128, 384)),
    ("gains", (128, 256)),
    ("a_wq", (DM, DM)), ("a_wk", (DM, DM)), ("a_wv", (DM, DM)), ("a_wo", (DM, DM)),
    ("a_lam", (128, 256)), ("a_subln", (128, 1)), ("pa_kT", (DM, NPAST)), ("pa_v", (NPAST, DM)),
    ("b_wq", (DM, DM)), ("b_wi", (DM, DM)), ("b_wf0", (DM, DM)), ("b_wf1", (DM, DM)), ("b_wg", (DM, DM)),
    ("b_wo", (DM, DM)), ("b_lowerT", (128, 128)), ("b_gnorm", (128, 1)), ("sb_f", (16, 128, 128)),
    ("sb_b", (16, 128, 128)), ("keep", (128, 1)),
    ("c_wdq", (DM, 512)), ("c_qnormT", (128, 4)), ("c_wuq", (512, 3072)), ("c_wdkv", (DM, 384)),
    ("c_kvnormT", (128, 2)), ("c_wuk", (256, DM)), ("c_wuv", (256, DM)), ("c_wo", (DM, DM)),
    ("pc_ckvT", (256, NPAST)), ("pc_kpeT2", (128, NPAST)),
    ("d_wq", (DM, DM)), ("d_wk", (DM, 512)), ("d_wv", (DM, 256)), ("d_wo", (DM, DM)), ("d_sinkP", (128, 32)),
    ("pd_kT2", (512, NPAST)), ("pd_v", (NPAST, 256)),
    ("ffn_wg", (4, DM, DFF)), ("ffn_wu", (4, DM, DFF)), ("ffn_wd", (4, DFF, DM)),
    ("ident", (128, 128)), ("perm", (128, 128)), ("maskq", (8, T)), ("maskk", (8, NK)), ("tri", (128, 1024)),
    ("ropeC", (128, T)), ("ropeS", (128, T)), ("bmask", (128, 256)), ("scanm", (128, T)), ("dbias", (128, 40)),
]
OUT_SPECS = [
    ("yT", (DM, T)), ("a_kT", (DM, T)), ("a_v", (T, DM)), ("b_f", (4, 16, 128, 128)), ("b_b", (4, 16, 128, 128)),
    ("c_ckvT", (256, T)), ("c_kpeT", (64, T)), ("d_kT", (256, T)), ("d_v", (T, 256)),
]


def build_program(n_layers=4, skip=()):
    nc = bass.Bass("TRN2", target_bir_lowering=False)
    class _LazyIn(dict):
        def __missing__(self, n):
            self[n] = nc.dram_tensor(n, list(dict(IN_SPECS)[n]), F32, kind="ExternalInput").ap()
            return self[n]
    I = _LazyIn()
    OUT = {n: nc.dram_tensor(n, list(s), F32, kind="ExternalOutput").ap() for n, s in OUT_SPECS}
    with ExitStack() as st:
        kb = KB(nc, st)

        def sb(name, shape, dt=F32):
            return st.enter_context(nc.sbuf_tensor(name, list(shape), dt))

        X = sb("X", [128, KC, T])
        dX = [Dep() for _ in range(KC)]
        RA, RB, RC = sb("RA", [128, 8192]), sb("RB", [128, 8192]), sb("RC", [128, 8192])
        dRA, dRB, dRC = [Dep() for _ in range(8)], [Dep() for _ in range(8)], [Dep() for _ in range(8)]
        RAb, RBb, RCb = RA[:].bitcast(BF16), RB[:].bitcast(BF16), RC[:].bitcast(BF16)
        Hv = RAb.rearrange("p (c t) -> p c t", c=KC)
        Ov = RBb.rearrange("p (c t) -> p c t", c=KC)
        dH = lambda c: dRA[c // 2]
        dO = lambda c: dRB[c // 2]
        RAf = RA[:].rearrange("p (c t) -> p c t", c=8)
        RBf = RB[:].rearrange("p (c t) -> p c t", c=8)
        RCf = RC[:].rearrange("p (c t) -> p c t", c=8)

        WB = [sb(f"WB{i}", [128, 4096], BF16) for i in range(2)]
        dWB = [Dep() for _ in range(2)]
        MS = sb("MS", [128, 4, 512])
        dMS = [Dep() for _ in range(4)]
        XBt = sb("XBt", [128, 512], BF16)
        dXB = Dep()
        RSTD = sb("RSTD", [128, 512])
        dRSTD = Dep()
        ROPEC, ROPES = sb("ROPEC", [128, T]), sb("ROPES", [128, T])
        ONES = sb("ONES", [128, 128], BF16)
        IDENT = sb("IDENT", [128, 128], BF16)
        PERM = sb("PERM", [128, 128], BF16)
        MQ = sb("MQ", [8, T], BF16)
        MK = sb("MK", [8, NK], BF16)
        TRI = sb("TRI", [128, 1024], BF16)
        BMASK = sb("BMASK", [128, 256], BF16)
        SCANM = TRI
        dTRI = Dep()
        GAINS = sb("GAINS", [128, 256])
        ADAB = sb("ADAB", [128, 96])
        MODT = sb("MODT", [128, 96])
        GSM, SHM, GGM = sb("GSM", [128, 16]), sb("SHM", [128, 16]), sb("GGM", [128, 16])
        GSF, SHF, GGF = sb("GSF", [128, 16]), sb("SHF", [128, 16]), sb("GGF", [128, 16])
        SC = sb("SC", [128, 16], BF16)
        SMALL = sb("SMALL", [128, 64])
        ESINK = sb("ESINK", [128, 32])
        DBIAS = sb("DBIAS", [128, 40])
        LBT = sb("LBT", [128, 128])
        EBL = sb("EBL", [128, 32])
        SST = sb("SST", [128, 128])
        SSTb = sb("SSTb", [128, 128], BF16)
        dC = Dep()
        dMOD = Dep()
        dSMALL = Dep()
        dEBL = Dep()
        dSST = Dep()
        dSSTb = Dep()

        PS = [st.enter_context(nc.psum_tensor(f"PS{i}", [128, 512], F32)) for i in range(8)]
        dPS = [Dep(excl=True) for _ in range(8)]
        rr = [0]

        def bank():
            rr[0] = (rr[0] + 1) % 8
            return rr[0]

        evq = [0]

        def ev_eng():
            evq[0] ^= 1
            return "act" if evq[0] else "dve"

        def copy_op(en, out, in_, rd, wr):
            if en == "act":
                return kb.op("act", lambda e: e.activation(out=out, in_=in_, func=AF.Identity, bias=0.0, scale=1.0), rd=rd, wr=wr)
            return kb.op("dve", lambda e: e.tensor_copy(out=out, in_=in_), rd=rd, wr=wr)

        class WS:
            def __init__(self):
                self.req = []
                self.i = 0
                self.issued = 0
                self.released = set()

            def _issue(self, j):
                view, kc, ncol = self.req[j]
                buf = WB[j % 2][:, 0:kc * ncol].rearrange("p (k n) -> p k n", k=kc)
                kb.dma("pool", buf, view.rearrange("(k p) n -> p k n", p=128), wr=[dWB[j % 2]])

            def _pump(self):
                while (self.issued < len(self.req) and self.issued < self.i + 2
                       and (self.issued < 2 or (self.issued - 2) in self.released)):
                    self._issue(self.issued)
                    self.issued += 1

            def next(self, view, kc, ncol):
                if kb.dry:
                    self.req.append((view, kc, ncol))
                    return WB[0][:, 0:kc * ncol].rearrange("p (k n) -> p k n", k=kc), dWB[0], -1
                j = self.i
                self.i += 1
                self._pump()
                assert self.issued > j, "weight stream: more than 2 blocks alive"
                _, kc2, ncol2 = self.req[j]
                assert (kc2, ncol2) == (kc, ncol)
                return WB[j % 2][:, 0:kc * ncol].rearrange("p (k n) -> p k n", k=kc), dWB[j % 2], j

            def release(self, j):
                if kb.dry:
                    return
                self.released.add(j)
                self._pump()

        ws = WS()

        def linear_fm(wview, kc_n, ncols, src, src_deps, consume, ntok=T, blk=256):
            for b0 in range(0, ncols, blk):
                bn = min(blk, ncols - b0)
                tile, dep, wid = ws.next(wview[:, b0:b0 + bn], kc_n, bn)
                pend = []
                for j in range(bn // 128):
                    oc = (b0 // 128) + j
                    for t0 in range(0, ntok, 512):
                        tn = min(512, ntok - t0)
                        bk = bank()
                        fns = [(lambda e, k=k, bk=bk, j=j, t0=t0, tn=tn: e.matmul(
                            PS[bk][:, 0:tn], lhsT=tile[:, k, j * 128:(j + 1) * 128], rhs=src(k, t0, tn),
                            start=(k == 0), stop=(k == kc_n - 1))) for k in range(kc_n)]
                        kb.pe_group(fns, rd=[dep] + src_deps, wr=[dPS[bk]])
                        if j == bn // 128 - 1 and t0 + 512 >= ntok:
                            ws.release(wid)
                        consume(oc, t0, tn, bk)

        def linear_tm(wview, kc_n, ncols, src, src_deps, consume, ntiles=8, blk=256):
            for b0 in range(0, ncols, blk):
                bn = min(blk, ncols - b0)
                tile, dep, wid = ws.next(wview[:, b0:b0 + bn], kc_n, bn)
                for tt in range(ntiles):
                    bk = bank()
                    fns = [(lambda e, k=k, bk=bk, tt=tt: e.matmul(
                        PS[bk][:, 0:bn], lhsT=src(k, tt * 128, 128), rhs=tile[:, k, :],
                        start=(k == 0), stop=(k == kc_n - 1))) for k in range(kc_n)]
                    kb.pe_group(fns, rd=[dep] + src_deps, wr=[dPS[bk]])
                    if tt == ntiles - 1:
                        ws.release(wid)
                    consume(tt, b0, bn, bk)

        hsrc = lambda k, t0, tn: Hv[:, k, t0:t0 + tn]
        osrc = lambda k, t0, tn: Ov[:, k, t0:t0 + tn]

        def load_consts():
            kb.dma("sp", ROPEC[:], I["ropeC"], wr=[dC])
            kb.dma("sp", ROPES[:], I["ropeS"], wr=[dC])
            kb.dma("pool", BMASK[:], I["bmask"], wr=[dC])
            kb.dma("sp", GAINS[:], I["gains"], wr=[dC])
            kb.dma("sp", ESINK[:], I["d_sinkP"], wr=[dC])
            kb.dma("sp", DBIAS[:], I["dbias"], wr=[dC])
            kb.dma("sp", LBT[:], I["b_lowerT"], wr=[dC])
            kb.dma("pool", IDENT[:], I["ident"], wr=[dC])
            kb.dma("pool", PERM[:], I["perm"], wr=[dC])
            kb.dma("pool", MQ[:], I["maskq"], wr=[dC])
            kb.dma("pool", MK[:], I["maskk"], wr=[dC])
            kb.op("dve", lambda e: e.memset(ONES[:], 1.0), wr=[dC])
            kb.dma("sp", SMALL[:, 0:16], I["condT"], wr=[dSMALL])
            kb.op("act", lambda e: e.activation(out=SC[:], in_=SMALL[:, 0:16], func=AF.Silu), rd=[dSMALL], wr=[dC])
            kb.op("act", lambda e: e.activation(out=ESINK[:], in_=ESINK[:], func=AF.Exp), rd=[dC], wr=[dC])
            for c in range(KC):
                kb.dma("sp", X[:, c, :], I["xT"][c * 128:(c + 1) * 128, :], wr=[dX[c]])

        def ada(l):
            kb.dma("sp", ADAB[:], I["ada_bT"][:, l * 96:(l + 1) * 96], wr=[dMOD])
            SCB = MS[:, 0:2, :].rearrange("p a b -> p (a b)").bitcast(BF16).rearrange("p (k m) -> p k m", k=KC)
            kb.op("dve", lambda e: e.tensor_copy(out=SCB, in_=SC[:, :].unsqueeze(2).to_broadcast([128, KC, 128])),
                  rd=[dC], wr=[dMS[0], dMS[1]])
            wv = I["ada_w"][l]
            tmp = MS[:, 2, 0:256].rearrange("p (j q) -> p j q", j=2)
            for b in range(48):
                tile, dep, wid = ws.next(wv[:, b * 256:(b + 1) * 256], KC, 256)
                bk = bank()
                fns = [(lambda e, k=k, bk=bk: e.matmul(PS[bk][:, 0:256], lhsT=SCB[:, k, :], rhs=tile[:, k, :],
                                                       start=(k == 0), stop=(k == KC - 1))) for k in range(KC)]
                kb.pe_group(fns, rd=[dep, dMS[0], dMS[1]], wr=[dPS[bk]])
                ws.release(wid)
                kb.op("dve", lambda e, bk=bk: e.tensor_tensor(out=tmp, in0=PS[bk][:, 0:256].rearrange("p (j q) -> p j q", j=2),
                                                              in1=IDENT[:, :].unsqueeze(1).to_broadcast([128, 2, 128]), op=ALU.mult),
                      rd=[dPS[bk], dC], wr=[dMS[2]])
                kb.op("dve", lambda e, b=b: e.tensor_reduce(out=MODT[:, 2 * b:2 * b + 2], in_=tmp, axis=AX.X, op=ALU.add),
                      rd=[dMS[2]], wr=[dMOD])
            kb.op("dve", lambda e: e.tensor_tensor(out=MODT[:], in0=MODT[:], in1=ADAB[:], op=ALU.add),
                  rd=[dMOD], wr=[dMOD])
            g = lambda n: GAINS[:, n * 64 + l * 16:n * 64 + l * 16 + 16]
            kb.op("dve", lambda e: e.scalar_tensor_tensor(out=GSM[:], in0=MODT[:, 16:32], scalar=1.0, in1=g(0), op0=ALU.add, op1=ALU.mult), rd=[dMOD, dC], wr=[dMOD])
            kb.op("dve", lambda e: e.tensor_copy(out=SHM[:], in_=MODT[:, 0:16]), rd=[dMOD], wr=[dMOD])
            kb.op("dve", lambda e: e.tensor_tensor(out=GGM[:], in0=MODT[:, 32:48], in1=g(1), op=ALU.mult), rd=[dMOD, dC], wr=[dMOD])
            kb.op("dve", lambda e: e.scalar_tensor_tensor(out=GSF[:], in0=MODT[:, 64:80], scalar=1.0, in1=g(2), op0=ALU.add, op1=ALU.mult), rd=[dMOD, dC], wr=[dMOD])
            kb.op("dve", lambda e: e.tensor_copy(out=SHF[:], in_=MODT[:, 48:64]), rd=[dMOD], wr=[dMOD])
            kb.op("dve", lambda e: e.tensor_tensor(out=GGF[:], in0=MODT[:, 80:96], in1=g(3), op=ALU.mult), rd=[dMOD, dC], wr=[dMOD])

        def sq_buf(i):
            return MS[:, 2 + (i % 2), :].bitcast(BF16)[:, 0:512], dMS[2 + (i % 2)]

        def rstd_half(src, sdeps, nch, dtot, t0, bk=None):
            bk = bank() if bk is None else bk
            for c in range(nch):
                sq, dsq = sq_buf(c)
                kb.op("act", lambda e, c=c, sq=sq: e.activation(out=sq, in_=src(c, t0, 512), func=AF.Square),
                      rd=[sdeps(c)], wr=[dsq])
                kb.op("pe", lambda e, c=c, sq=sq: e.matmul(PS[bk][:, :], lhsT=ONES[:], rhs=sq, start=(c == 0), stop=(c == nch - 1)),
                      rd=[dsq, dC], wr=[dPS[bk]] if c in (0, nch - 1) else [])
            kb.op("act", lambda e: e.activation(out=RSTD[:], in_=PS[bk][:, :], func=AF.Sqrt, bias=EPSB[:, 0:1], scale=1.0 / dtot),
                  rd=[dPS[bk], dC], wr=[dRSTD])
            kb.op("dve", lambda e: e.reciprocal(out=RSTD[:], in_=RSTD[:]), rd=[dRSTD], wr=[dRSTD])

        EPSB = sb("EPSB", [128, 1])

        def norm_mod(gs, sh):
            for t0 in (0, 512):
                rstd_half(lambda c, a, n: X[:, c, a:a + n], lambda c: dX[c], KC, DM, t0)
                for c in range(KC):
                    tmp, dt_ = MS[:, c % 2, :], dMS[c % 2]
                    kb.op("dve", lambda e, c=c, tmp=tmp: e.tensor_tensor(out=tmp, in0=X[:, c, t0:t0 + 512], in1=RSTD[:], op=ALU.mult),
                          rd=[dX[c], dRSTD], wr=[dt_])
                    kb.op("act", lambda e, c=c, tmp=tmp: e.activation(out=Hv[:, c, t0:t0 + 512], in_=tmp, func=AF.Identity,
                                                                      bias=sh[:, c:c + 1], scale=gs[:, c:c + 1]),
                          rd=[dt_, dMOD], wr=[dH(c)])

        def post_norm_residual(ysrc, ydep, gg):
            for t0 in (0, 512):
                rstd_half(ysrc, ydep, KC, DM, t0)
                for c in range(KC):
                    tmp, dt_ = MS[:, c % 2, :], dMS[c % 2]
                    kb.op("dve", lambda e, c=c, tmp=tmp: e.tensor_tensor(out=tmp, in0=ysrc(c, t0, 512), in1=RSTD[:], op=ALU.mult),
                          rd=[ydep(c), dRSTD], wr=[dt_])
                    kb.op("dve", lambda e, c=c, tmp=tmp: e.scalar_tensor_tensor(out=X[:, c, t0:t0 + 512], in0=tmp, scalar=gg[:, c:c + 1],
                                                                                in1=X[:, c, t0:t0 + 512], op0=ALU.mult, op1=ALU.add),
                          rd=[dt_, dMOD], wr=[dX[c]])

        def ymix(c, a, n):
            return (RAf if c < 8 else RCf)[:, c % 8, a:a + n]

        def ymix_dep(c):
            return (dRA if c < 8 else dRC)[c % 8]

        def yffn(c, a, n):
            return (RBf if c < 8 else RCf)[:, c % 8, a:a + n]

        def yffn_dep(c):
            return (dRB if c < 8 else dRC)[c % 8]

        def out_proj(wview, scratch_deps):
            kb.retire(scratch_deps, dRC)

            def cons(oc, t0, tn, bk):
                copy_op(ev_eng(), ymix(oc, t0, tn), PS[bk][:, 0:tn], [dPS[bk]], [ymix_dep(oc)])
            linear_fm(wview, KC, DM, osrc, [dO(c) for c in range(0, KC, 2)], cons)
            post_norm_residual(ymix, ymix_dep, GGM)

        def rope_evac(bk, t0, tn, out_ap, out_deps, np_=128):
            kb.op("act", lambda e: e.activation(out=XBt[0:np_, 0:tn], in_=PS[bk][0:np_, 0:tn], func=AF.Identity, bias=0.0, scale=1.0), rd=[dPS[bk]], wr=[dXB])
            b2 = bank()
            kb.pe_group([lambda e: e.matmul(PS[b2][0:np_, 0:tn], lhsT=PERM[0:np_, 0:np_], rhs=XBt[0:np_, 0:tn], start=True, stop=True)],
                        rd=[dXB, dC], wr=[dPS[b2]])
            t1, t2 = MS[0:np_, 0, 0:tn], MS[0:np_, 1, 0:tn]
            kb.op("dve", lambda e: e.tensor_tensor(out=t1, in0=PS[bk][0:np_, 0:tn], in1=ROPEC[0:np_, t0:t0 + tn], op=ALU.mult),
                  rd=[dPS[bk], dC], wr=[dMS[0]])
            kb.op("dve", lambda e: e.tensor_tensor(out=t2, in0=PS[b2][0:np_, 0:tn], in1=ROPES[0:np_, t0:t0 + tn], op=ALU.mult),
                  rd=[dPS[b2], dC], wr=[dMS[1]])
            kb.op("dve", lambda e: e.tensor_tensor(out=out_ap, in0=t1, in1=t2, op=ALU.add), rd=[dMS[0], dMS[1]], wr=out_deps)

        def rope_evac2(bk, t0, tn, out_lo, out_hi, out_deps):
            if "r3" in skip:
                copy_op(ev_eng(), out_lo, PS[bk][0:64, 0:tn], [dPS[bk]], out_deps)
                return
            if "r4" in skip:
                kb.op("act", lambda e: e.activation(out=XBt[:, 0:tn], in_=PS[bk][:, 0:tn], func=AF.Identity, bias=0.0, scale=1.0), rd=[dPS[bk]], wr=[dXB])
                return
            if "r6" in skip:
                kb.op("act", lambda e: e.activation(out=MS[:, 2, 0:tn], in_=PS[bk][:, 0:tn], func=AF.Identity, bias=0.0, scale=1.0), rd=[dPS[bk]], wr=[dMS[2]])
                return
            if "r7" in skip:
                kb.op("act", lambda e: e.activation(out=XBt[:, 0:tn], in_=PS[bk][:, 0:tn], func=AF.Identity, bias=0.0, scale=1.0), rd=[dPS[bk]], wr=[dXB])
                return
            if "r8" in skip:
                kb.op("dve", lambda e: e.tensor_copy(out=XBt[:, 0:tn], in_=PS[bk][:, 0:tn]), rd=[dPS[bk]], wr=[dXB])
                return
            if "r5" in skip:
                kb.op("dve", lambda e: e.tensor_tensor(out=MS[:, 0, 0:tn], in0=PS[bk][:, 0:tn], in1=ROPEC[:, t0:t0 + tn], op=ALU.mult),
                      rd=[dPS[bk], dC], wr=[dMS[0]])
                return
            kb.op("act", lambda e: e.activation(out=XBt[:, 0:tn], in_=PS[bk][:, 0:tn], func=AF.Identity, bias=0.0, scale=1.0), rd=[dPS[bk]], wr=[dXB])
            b2 = bank()
            if "r2" not in skip:
                kb.pe_group([lambda e: e.matmul(PS[b2][:, 0:tn], lhsT=PERM[:, :], rhs=XBt[:, 0:tn], start=True, stop=True)],
                            rd=[dXB, dC], wr=[dPS[b2]])
            else:
                b2 = bk
            t1, t2 = MS[:, 0, 0:tn], MS[:, 1, 0:tn]
            kb.op("dve", lambda e: e.tensor_tensor(out=t1, in0=PS[bk][:, 0:tn], in1=ROPEC[:, t0:t0 + tn], op=ALU.mult),
                  rd=[dPS[bk], dC] + ([dXB] if "r11" in skip else []), wr=[dMS[0]])
            if "r10" not in skip:
                kb.op("dve", lambda e: e.tensor_tensor(out=t2, in0=PS[b2][:, 0:tn], in1=ROPES[:, t0:t0 + tn], op=ALU.mult),
                      rd=[dPS[b2], dC], wr=[dMS[1]])
            else:
                t2 = t1
            if "r9" in skip:
                return
            kb.op("dve", lambda e: e.tensor_tensor(out=out_lo, in0=t1[0:64, :], in1=t2[0:64, :], op=ALU.add), rd=[dMS[0], dMS[1]], wr=out_deps)
            if "r1" not in skip:
                kb.op("dve", lambda e: e.tensor_tensor(out=out_hi, in0=t1[64:128, :], in1=t2[64:128, :], op=ALU.add), rd=[dMS[0], dMS[1]], wr=out_deps)

        def stage_out(bk, rows, tn, dram_ap, idx):
            stg, dstg = MS[:, 2 + idx % 2, :], dMS[2 + idx % 2]
            copy_op(ev_eng(), stg[0:rows, 0:tn], PS[bk][0:rows, 0:tn], [dPS[bk]], [dstg])
            kb.dma("sp", dram_ap, stg[0:rows, 0:tn], rd=[dstg])

        def exp_tile(sbk, ebuf, edep, scale, n=512):
            kb.op("act", lambda e: e.activation(out=ebuf, in_=PS[sbk][:, 0:n], func=AF.Exp, scale=scale), rd=[dPS[sbk]], wr=[edep])

        def mixer_a(layer_idx):
            lam_init = 0.8 - 0.6 * math.exp(-0.3 * layer_idx)
            LAMT = MS[:, 0, 0:256]
            kb.dma("sp", LAMT, I["a_lam"], wr=[dMS[0]])
            kb.dma("sp", SMALL[:, 16:17], I["a_subln"], wr=[dSMALL])
            pr = MS[:, 1, 0:128]
            kb.op("dve", lambda e: e.tensor_tensor(out=pr[:, 0:64], in0=LAMT[:, 0:64], in1=LAMT[:, 64:128], op=ALU.mult), rd=[dMS[0]], wr=[dMS[1]])
            kb.op("dve", lambda e: e.tensor_tensor(out=pr[:, 64:128], in0=LAMT[:, 128:192], in1=LAMT[:, 192:256], op=ALU.mult), rd=[dMS[0]], wr=[dMS[1]])
            kb.op("dve", lambda e: e.tensor_reduce(out=SMALL[:, 17:18], in_=pr[:, 0:64], axis=AX.X, op=ALU.add), rd=[dMS[1]], wr=[dSMALL])
            kb.op("dve", lambda e: e.tensor_reduce(out=SMALL[:, 18:19], in_=pr[:, 64:128], axis=AX.X, op=ALU.add), rd=[dMS[1]], wr=[dSMALL])
            kb.op("act", lambda e: e.activation(out=SMALL[:, 17:19], in_=SMALL[:, 17:19], func=AF.Exp), rd=[dSMALL], wr=[dSMALL])
            kb.op("dve", lambda e: e.tensor_tensor(out=SMALL[:, 19:20], in0=SMALL[:, 18:19], in1=SMALL[:, 17:18], op=ALU.subtract), rd=[dSMALL], wr=[dSMALL])
            kb.op("dve", lambda e: e.tensor_scalar_add(out=SMALL[:, 19:20], in0=SMALL[:, 19:20], scalar1=-lam_init), rd=[dSMALL], wr=[dSMALL])
            kb.op("dve", lambda e: e.tensor_scalar_mul(out=SMALL[:, 20:21], in0=SMALL[:, 16:17], scalar1=(1.0 - lam_init)), rd=[dSMALL], wr=[dSMALL])
            NEGLAM, SUBG = SMALL[:, 19:20], SMALL[:, 20:21]

            QT = RCb[:, 0:4096].rearrange("p (j t) -> p j t", j=4)
            KT = RCb[:, 4096:9216].rearrange("p (j t) -> p j t", j=4)
            VT = RCb[:, 9216:11776].rearrange("p (k c) -> p k c", k=10)
            Eb = [RCb[:, 11776 + i * 512:11776 + (i + 1) * 512] for i in range(4)]
            OH = RC[:, 6912:6912 + 512]
            dQ, dK, dV, dOH = Dep(), Dep(), Dep(), Dep()
            dE = [Dep() for _ in range(4)]
            sdeps = [dQ, dK, dV, dOH] + dE
            kb.retire(dRC, sdeps)
            scale = 64 ** -0.5
            for hp in range(8):
                def cq(oc, t0, tn, bk):
                    o_ = oc - 2 * hp
                    rope_evac2(bk, t0, tn, QT[0:64, 2 * o_, t0:t0 + tn], QT[0:64, 2 * o_ + 1, t0:t0 + tn], [dQ])
                if "a_q" not in skip:
                    linear_fm(I["a_wq"][:, hp * 256:(hp + 1) * 256], KC, 256, hsrc, [dH(c) for c in range(0, KC, 2)],
                              lambda oc, t0, tn, bk: cq(oc + 2 * hp, t0, tn, bk))

                def ck(oc, t0, tn, bk):
                    h = 2 * hp + oc
                    stage_out(bk, 128, tn, OUT["a_kT"][h * 128:(h + 1) * 128, t0:t0 + tn], oc)
                    rope_evac2(bk, t0, tn, KT[0:64, 2 * oc, NPAST + t0:NPAST + t0 + tn], KT[0:64, 2 * oc + 1, NPAST + t0:NPAST + t0 + tn], [dK])
                if "a_k" not in skip:
                    linear_fm(I["a_wk"][:, hp * 256:(hp + 1) * 256], KC, 256, hsrc, [dH(c) for c in range(0, KC, 2)], ck)
                for j in (range(2) if "a_past" not in skip else ()):
                    h = 2 * hp + j
                    for m in range(2):
                        kb.dma("pool", KT[0:64, 2 * j + m, 0:NPAST], I["pa_kT"][h * 128 + m * 64:h * 128 + (m + 1) * 64, :], wr=[dK])

                def cv(tt, c0, cn, bk):
                    stage_out(bk, 128, 256, OUT["a_v"][tt * 128:(tt + 1) * 128, hp * 256:(hp + 1) * 256], tt)
                    copy_op("act", VT[:, 2 + tt, :], PS[bk][:, 0:256], [dPS[bk]], [dV])
                if "a_v" not in skip:
                    linear_tm(I["a_wv"][:, hp * 256:(hp + 1) * 256], KC, 256, hsrc, [dH(c) for c in range(0, KC, 2)], cv)
                if "a_past" not in skip:
                  kb.dma("pool", VT[:, 0:2, :], I["pa_v"][:, hp * 256:(hp + 1) * 256].rearrange("(k p) c -> p k c", p=128), wr=[dV])
                for j in (range(2) if "a_attn" not in skip else ()):
                    h = 2 * hp + j
                    for t0 in (0, 512):
                        for kt in range(10):
                            for m in range(2):
                                sbk = m * 2 + (kt % 2)
                                ei = m * 2 + (kt % 2)
                                kb.pe_group([
                                    lambda e, m=m, kt=kt, sbk=sbk: e.matmul(PS[sbk][:, :], lhsT=KT[0:64, 2 * j + m, kt * 128:(kt + 1) * 128],
                                                                            rhs=QT[0:64, 2 * j + m, t0:t0 + 512], start=True, stop=False),
                                    lambda e, kt=kt, sbk=sbk: e.matmul(PS[sbk][:, :], lhsT=MK[0:8, kt * 128:(kt + 1) * 128], rhs=MQ[0:8, t0:t0 + 512],
                                                                       start=False, stop=True)],
                                    rd=[dQ, dK, dC], wr=[dPS[sbk]])
                                exp_tile(sbk, Eb[ei], dE[ei], scale)
                                ob, sbn = 4 + m, 6 + m
                                first, last = kt == 0, kt == 9
                                kb.op("pe", lambda e, kt=kt, ei=ei, ob=ob: e.matmul(PS[ob][:, :], lhsT=VT[:, kt, j * 128:(j + 1) * 128], rhs=Eb[ei],
                                                                                    start=(kt == 0), stop=(kt == 9)),
                                      rd=[dV, dE[ei]], wr=[dPS[ob]] if (first or last) else [])
                                kb.op("pe", lambda e, kt=kt, ei=ei, sbn=sbn: e.matmul(PS[sbn][:, :], lhsT=ONES[:], rhs=Eb[ei], start=(kt == 0), stop=(kt == 9)),
                                      rd=[dE[ei], dC], wr=[dPS[sbn]] if (first or last) else [])
                        r1, t1, r2, t2 = MS[:, 0, :], MS[:, 1, :], MS[:, 2, :], MS[:, 3, :]
                        kb.op("dve", lambda e: e.reciprocal(out=r1, in_=PS[6][:, :]), rd=[dPS[6]], wr=[dMS[0]])
                        kb.op("dve", lambda e: e.tensor_tensor(out=t1, in0=PS[4][:, :], in1=r1, op=ALU.mult), rd=[dPS[4], dMS[0]], wr=[dMS[1]])
                        kb.op("dve", lambda e: e.reciprocal(out=r2, in_=PS[7][:, :]), rd=[dPS[7]], wr=[dMS[2]])
                        kb.op("dve", lambda e: e.tensor_tensor(out=t2, in0=PS[5][:, :], in1=r2, op=ALU.mult), rd=[dPS[5], dMS[2]], wr=[dMS[3]])
                        kb.op("dve", lambda e: e.scalar_tensor_tensor(out=OH, in0=t2, scalar=NEGLAM, in1=t1, op0=ALU.mult, op1=ALU.add),
                              rd=[dMS[1], dMS[3], dSMALL], wr=[dOH])
                        sq = MS[:, 0, :].bitcast(BF16)[:, 0:512]
                        kb.op("act", lambda e: e.activation(out=sq, in_=OH, func=AF.Square), rd=[dOH], wr=[dMS[0]])
                        kb.pe_group([lambda e: e.matmul(PS[0][:, :], lhsT=ONES[:], rhs=sq, start=True, stop=True)], rd=[dMS[0], dC], wr=[dPS[0]])
                        kb.op("act", lambda e: e.activation(out=r2, in_=PS[0][:, :], func=AF.Sqrt, bias=EPSB[:, 0:1], scale=1.0 / 128), rd=[dPS[0], dC], wr=[dMS[2]])
                        kb.op("dve", lambda e: e.reciprocal(out=r2, in_=r2), rd=[dMS[2]], wr=[dMS[2]])
                        kb.op("dve", lambda e: e.scalar_tensor_tensor(out=Ov[:, h, t0:t0 + 512], in0=OH, scalar=SUBG, in1=r2, op0=ALU.mult, op1=ALU.mult),
                              rd=[dOH, dMS[2], dSMALL], wr=[dO(h)])
            if "a_out" not in skip:
                out_proj(I["a_wo"], sdeps)

        def mixer_c():
            kb.dma("sp", SMALL[:, 24:28], I["c_qnormT"], wr=[dSMALL])
            kb.dma("sp", SMALL[:, 28:30], I["c_kvnormT"], wr=[dSMALL])
            QLN = RCb[:, 0:4096].rearrange("p (c t) -> p c t", c=4)
            CKVN = RCb[:, 4096:6656].rearrange("p (c t) -> p c t", c=2)
            KPE = RCb[:, 6656:7936]
            QN = RCb[:, 7936:9984].rearrange("p (j t) -> p j t", j=2)
            QR = RCb[:, 9984:12032].rearrange("p (j t) -> p j t", j=2)
            KN = RCb[:, 12032:13312]
            VT = RCb[:, 13312:14592].rearrange("p (k c) -> p k c", k=10)
            Eb = [RCb[:, 14592 + i * 512:14592 + (i + 1) * 512] for i in range(2)]
            QL = RC[:, 4096:8192].rearrange("p (c t) -> p c t", c=4)
            dQLN, dCKVN, dKPE, dQN, dQR, dKN, dV, dQL = [Dep() for _ in range(8)]
            dE = [Dep(), Dep()]
            sdeps = [dQLN, dCKVN, dKPE, dQN, dQR, dKN, dV, dQL] + dE
            kb.retire(dRC, sdeps)
            hd = [dH(c) for c in range(0, KC, 2)]
            linear_fm(I["c_wdq"], KC, 512, hsrc, hd, lambda oc, t0, tn, bk: copy_op(ev_eng(), QL[:, oc, t0:t0 + tn], PS[bk][:, 0:tn], [dPS[bk]], [dQL]))
            for t0 in (0, 512):
                rstd_half(lambda c, a, n: QL[:, c, a:a + n], lambda c: dQL, 4, 512, t0)
                for c in range(4):
                    tmp, dt_ = MS[:, c % 2, :], dMS[c % 2]
                    kb.op("dve", lambda e, c=c, tmp=tmp: e.tensor_tensor(out=tmp, in0=QL[:, c, t0:t0 + 512], in1=RSTD[:], op=ALU.mult), rd=[dQL, dRSTD], wr=[dt_])
                    kb.op("act", lambda e, c=c, tmp=tmp: e.activation(out=QLN[:, c, t0:t0 + 512], in_=tmp, func=AF.Identity, bias=0.0, scale=SMALL[:, 24 + c:25 + c]),
                          rd=[dt_, dSMALL], wr=[dQLN])
            CKV = RC[:, 4096:6144].rearrange("p (c t) -> p c t", c=2)
            dCKV = dQL

            def ckv_cons(oc, t0, tn, bk):
                if oc < 2:
                    copy_op(ev_eng(), CKV[:, oc, t0:t0 + tn], PS[bk][:, 0:tn], [dPS[bk]], [dCKV])
                else:
                    stage_out(bk, 64, tn, OUT["c_kpeT"][:, t0:t0 + tn], t0 // 512)
                    rope_evac(bk, t0, tn, KPE[0:64, NPAST + t0:NPAST + t0 + tn], [dKPE], np_=64)
            linear_fm(I["c_wdkv"], KC, 384, hsrc, hd, ckv_cons)
            kb.dma("pool", KPE[0:64, 0:NPAST], I["pc_kpeT2"][0:64, :], wr=[dKPE])
            for c in range(2):
                kb.dma("pool", CKVN[:, c, 0:NPAST], I["pc_ckvT"][c * 128:(c + 1) * 128, :], wr=[dCKVN])
            for t0 in (0, 512):
                rstd_half(lambda c, a, n: CKV[:, c, a:a + n], lambda c: dCKV, 2, 256, t0)
                for c in range(2):
                    tmp, dt_ = MS[:, c % 2, :], dMS[c % 2]
                    kb.op("dve", lambda e, c=c, tmp=tmp: e.tensor_tensor(out=tmp, in0=CKV[:, c, t0:t0 + 512], in1=RSTD[:], op=ALU.mult), rd=[dCKV, dRSTD], wr=[dt_])
                    stg, dstg = MS[:, 2 + c % 2, :], dMS[2 + c % 2]
                    kb.op("act", lambda e, c=c, tmp=tmp, stg=stg: e.activation(out=stg, in_=tmp, func=AF.Identity, bias=0.0, scale=SMALL[:, 28 + c:29 + c]),
                          rd=[dt_, dSMALL], wr=[dstg])
                    kb.dma("sp", OUT["c_ckvT"][c * 128:(c + 1) * 128, t0:t0 + 512], stg, rd=[dstg])
                    kb.op("dve", lambda e, c=c, stg=stg: e.tensor_copy(out=CKVN[:, c, NPAST + t0:NPAST + t0 + 512], in_=stg), rd=[dstg], wr=[dCKVN])
            kb.retire([dQL], [dQN, dQR, dKN, dV] + dE)
            scale = 192 ** -0.5
            qsrc = lambda k, t0, tn: QLN[:, k, t0:t0 + tn]
            csrc = lambda k, t0, tn: CKVN[:, k, t0:t0 + tn]
            for hp in range(8):
                linear_fm(I["c_wuq"][:, hp * 256:(hp + 1) * 256], 4, 256, qsrc, [dQLN],
                          lambda oc, t0, tn, bk: copy_op(ev_eng(), QN[:, oc, t0:t0 + tn], PS[bk][:, 0:tn], [dPS[bk]], [dQN]))
                linear_fm(I["c_wuq"][:, 2048 + hp * 128:2048 + (hp + 1) * 128], 4, 128, qsrc, [dQLN],
                          lambda oc, t0, tn, bk: rope_evac2(bk, t0, tn, QR[0:64, 0, t0:t0 + tn], QR[0:64, 1, t0:t0 + tn], [dQR]))
                tk, dtk, wk_id = ws.next(I["c_wuk"][:, hp * 256:(hp + 1) * 256], 2, 256)
                tv, dtv, wv_id = ws.next(I["c_wuv"][:, hp * 256:(hp + 1) * 256], 2, 256)
                for j in range(2):
                    h = 2 * hp + j
                    for t0, tn in ((0, 512), (512, 512), (1024, 256)):
                        bk = bank()
                        kb.pe_group([(lambda e, k=k, bk=bk, t0=t0, tn=tn: e.matmul(PS[bk][:, 0:tn], lhsT=tk[:, k, j * 128:(j + 1) * 128], rhs=CKVN[:, k, t0:t0 + tn],
                                                                                   start=(k == 0), stop=(k == 1))) for k in range(2)],
                                    rd=[dtk, dCKVN], wr=[dPS[bk]])
                        copy_op(ev_eng(), KN[:, t0:t0 + tn], PS[bk][:, 0:tn], [dPS[bk]], [dKN])
                    for kt in range(10):
                        bk = bank()
                        kb.pe_group([(lambda e, k=k, bk=bk, kt=kt: e.matmul(PS[bk][:, 0:128], lhsT=CKVN[:, k, kt * 128:(kt + 1) * 128], rhs=tv[:, k, j * 128:(j + 1) * 128],
                                                                             start=(k == 0), stop=(k == 1))) for k in range(2)],
                                    rd=[dtv, dCKVN], wr=[dPS[bk]])
                        copy_op(ev_eng(), VT[:, kt, :], PS[bk][:, 0:128], [dPS[bk]], [dV])
                    for t0 in (0, 512):
                        for kt in range(10):
                            sbk = kt % 2
                            kb.pe_group([
                                lambda e, kt=kt, sbk=sbk: e.matmul(PS[sbk][:, :], lhsT=KN[:, kt * 128:(kt + 1) * 128], rhs=QN[:, j, t0:t0 + 512], start=True, stop=False),
                                lambda e, kt=kt, sbk=sbk: e.matmul(PS[sbk][:, :], lhsT=KPE[0:64, kt * 128:(kt + 1) * 128],
                                                                   rhs=QR[0:64, j, t0:t0 + 512], start=False, stop=False),
                                lambda e, kt=kt, sbk=sbk: e.matmul(PS[sbk][:, :], lhsT=MK[0:8, kt * 128:(kt + 1) * 128], rhs=MQ[0:8, t0:t0 + 512], start=False, stop=True)],
                                rd=[dKN, dQN, dKPE, dQR, dC], wr=[dPS[sbk]])
                            exp_tile(sbk, Eb[kt % 2], dE[kt % 2], scale)
                            fl = kt in (0, 9)
                            kb.op("pe", lambda e, kt=kt: e.matmul(PS[4][:, :], lhsT=VT[:, kt, :], rhs=Eb[kt % 2], start=(kt == 0), stop=(kt == 9)),
                                  rd=[dV, dE[kt % 2]], wr=[dPS[4]] if fl else [])
                            kb.op("pe", lambda e, kt=kt: e.matmul(PS[6][:, :], lhsT=ONES[:], rhs=Eb[kt % 2], start=(kt == 0), stop=(kt == 9)),
                                  rd=[dE[kt % 2], dC], wr=[dPS[6]] if fl else [])
                        r1 = MS[:, 0, :]
                        kb.op("dve", lambda e: e.reciprocal(out=r1, in_=PS[6][:, :]), rd=[dPS[6]], wr=[dMS[0]])
                        kb.op("dve", lambda e: e.tensor_tensor(out=Ov[:, h, t0:t0 + 512], in0=PS[4][:, :], in1=r1, op=ALU.mult), rd=[dPS[4], dMS[0]], wr=[dO(h)])
                ws.release(wk_id)
                ws.release(wv_id)
            out_proj(I["c_wo"], sdeps)

        def mixer_d():
            KD = RCb[:, 0:5120].rearrange("p (g t) -> p g t", g=4)
            VT = RCb[:, 5120:7680].rearrange("p (k c) -> p k c", k=10)
            QD = RCb[:, 7680:15872].rearrange("p (c t) -> p c t", c=8)
            Eb = [MS[:, 2 + i, :].bitcast(BF16)[:, 0:512] for i in range(2)]
            dKD, dV, dQD = Dep(), Dep(), Dep()
            dE = [dMS[2], dMS[3]]
            sdeps = [dKD, dV, dQD] + dE
            kb.retire(dRC, sdeps)
            hd = [dH(c) for c in range(0, KC, 2)]
            kb.dma("pool", TRI[:], I["tri"], wr=[dTRI])

            def ck(oc, t0, tn, bk):
                stage_out(bk, 64, tn, OUT["d_kT"][oc * 64:(oc + 1) * 64, t0:t0 + tn], oc)
                rope_evac(bk, t0, tn, KD[0:64, oc, NPAST + t0:NPAST + t0 + tn], [dKD], np_=64)
            linear_fm(I["d_wk"], KC, 512, hsrc, hd, ck)
            for g in range(4):
                kb.dma("pool", KD[0:64, g, 0:NPAST], I["pd_kT2"][g * 128:g * 128 + 64, :], wr=[dKD])

            def cv(tt, c0, cn, bk):
                stage_out(bk, 128, 256, OUT["d_v"][tt * 128:(tt + 1) * 128, :], tt)
                copy_op("act", VT[:, 2 + tt, :], PS[bk][:, 0:256], [dPS[bk]], [dV])
            linear_tm(I["d_wv"], KC, 256, hsrc, hd, cv)
            kb.dma("pool", VT[:, 0:2, :], I["pd_v"].rearrange("(k p) c -> p k c", p=128), wr=[dV])
            scale = 64 ** -0.5
            for g in range(4):
                linear_fm(I["d_wq"][:, g * 512:(g + 1) * 512], KC, 512, hsrc, hd,
                          lambda oc, t0, tn, bk: rope_evac2(bk, t0, tn, QD[0:64, 2 * oc, t0:t0 + tn], QD[0:64, 2 * oc + 1, t0:t0 + tn], [dQD]))
                for qb in range(8):
                    q0 = qb * 128
                    kts = [(0, None), (1, None)] + [(2 + kb_, kb_ - qb) for kb_ in (qb - 1, qb, qb + 1) if 0 <= kb_ < 8]
                    for par in range(2):
                        p0 = par * 64
                        for i, (kt, off) in enumerate(kts):
                            sbk = i % 2
                            fns = [lambda e, kt=kt, sbk=sbk: e.matmul(PS[sbk][:, :].rearrange("p (j q) -> p j q", j=4), lhsT=KD[0:64, g, kt * 128:(kt + 1) * 128],
                                                                      rhs=QD[0:64, par * 4:(par + 1) * 4, q0:q0 + 128], start=True, stop=(off not in (-1, 1)))]
                            if off in (-1, 1):
                                tsel = 0 if off == -1 else 1
                                fns.append(lambda e, sbk=sbk, tsel=tsel: e.matmul(PS[sbk][:, :], lhsT=IDENT[:], rhs=TRI[:, tsel * 512:(tsel + 1) * 512],
                                                                                  start=False, stop=True))
                            kb.pe_group(fns, rd=[dKD, dQD, dC, dTRI], wr=[dPS[sbk]])
                            bcol = qb * 5 + i
                            kb.op("act", lambda e, sbk=sbk, i=i, bcol=bcol: e.activation(out=Eb[i % 2], in_=PS[sbk][:, :], func=AF.Exp, scale=scale,
                                                                                           bias=DBIAS[:, bcol:bcol + 1]),
                                  rd=[dPS[sbk], dC], wr=[dE[i % 2]])
                            fl = i in (0, len(kts) - 1)
                            kb.op("pe", lambda e, kt=kt, i=i: e.matmul(PS[4][0:64, :], lhsT=VT[:, kt, g * 64:(g + 1) * 64], rhs=Eb[i % 2],
                                                                       start=(i == 0), stop=(i == len(kts) - 1)),
                                  rd=[dV, dE[i % 2]], wr=[dPS[4]] if fl else [])
                            kb.op("pe", lambda e, i=i: e.matmul(PS[6][0:64, :], lhsT=ONES[:, 0:64], rhs=Eb[i % 2], start=(i == 0), stop=(i == len(kts) - 1)),
                                  rd=[dE[i % 2], dC], wr=[dPS[6]] if fl else [])
                        den = MS[0:64, 0, :]
                        es = ESINK[0:64, g * 8 + par * 4:g * 8 + par * 4 + 4].unsqueeze(2).to_broadcast([64, 4, 128])
                        kb.op("dve", lambda e: e.tensor_tensor(out=den.rearrange("p (j q) -> p j q", j=4), in0=PS[6][0:64, :].rearrange("p (j q) -> p j q", j=4),
                                                               in1=es, op=ALU.add), rd=[dPS[6], dC], wr=[dMS[0]])
                        kb.op("dve", lambda e: e.reciprocal(out=den, in_=den), rd=[dMS[0]], wr=[dMS[0]])
                        for e2 in range(2):
                            kb.op("dve", lambda e, e2=e2: e.tensor_tensor(
                                out=Ov[e2 * 64:(e2 + 1) * 64, g * 4 + 2 * par:g * 4 + 2 * par + 2, q0:q0 + 128],
                                in0=PS[4][0:64, :].rearrange("p (i two q) -> p i two q", two=2, q=128)[:, :, e2, :],
                                in1=den.rearrange("p (i two q) -> p i two q", two=2, q=128)[:, :, e2, :], op=ALU.mult),
                                rd=[dPS[4], dMS[0]], wr=[dO(g * 4 + 2 * par)])
            out_proj(I["d_wo"], sdeps)

        def mixer_b(layer_idx):
            kb.op("act", lambda e: e.activation(out=LBT[:], in_=LBT[:], func=AF.Exp), rd=[dC], wr=[dC])
            tot, lbv, oml = SMALL[:, 32:64], EBLB[:, 0:32], EBLB[:, 32:64]
            kb.op("dve", lambda e: e.tensor_tensor(out=tot, in0=LBT[:, 0:32], in1=LBT[:, 32:64], op=ALU.add), rd=[dC], wr=[dSMALL])
            kb.op("dve", lambda e: e.tensor_tensor(out=tot, in0=tot, in1=LBT[:, 64:96], op=ALU.add), rd=[dC, dSMALL], wr=[dSMALL])
            kb.op("dve", lambda e: e.tensor_tensor(out=tot, in0=tot, in1=LBT[:, 96:128], op=ALU.add), rd=[dC, dSMALL], wr=[dSMALL])
            kb.op("dve", lambda e: e.reciprocal(out=tot, in_=tot), rd=[dSMALL], wr=[dSMALL])
            kb.op("dve", lambda e: e.tensor_copy(out=lbv, in_=LBT[:, 32:64]), rd=[dC], wr=[dSMALL])
            for l in range(2, layer_idx + 1):
                kb.op("dve", lambda e, l=l: e.tensor_tensor(out=lbv, in0=lbv, in1=LBT[:, l * 32:(l + 1) * 32], op=ALU.add), rd=[dC, dSMALL], wr=[dSMALL])
            kb.op("dve", lambda e: e.tensor_tensor(out=lbv, in0=lbv, in1=tot, op=ALU.mult), rd=[dSMALL], wr=[dSMALL])
            kb.op("dve", lambda e: e.tensor_scalar(out=oml, in0=lbv, scalar1=-1.0, scalar2=1.0, op0=ALU.mult, op1=ALU.add), rd=[dSMALL], wr=[dSMALL])
            kb.dma("sp", SMALL[:, 21:22], I["b_gnorm"], wr=[dSMALL])
            kb.dma("sp", SMALL[:, 22:23], I["keep"], wr=[dSMALL])
            GN, KEEP = SMALL[:, 21:22], SMALL[:, 22:23]
            kb.dma("pool", SCANM[:], I["scanm"], wr=[dTRI])
            W1, W2, W3, OACC = RC[:, 0:1024], RC[:, 1024:2048], RC[:, 2048:3072], RC[:, 3072:4096]
            QS, SG = RCb[:, 8192:9216], RCb[:, 9216:10240]
            VT = RCb[:, 10240:11264].rearrange("p (k c) -> p k c", k=8)
            QDEC, KINV, KENDT = RCb[:, 11264:12288], RCb[:, 12288:13312], RCb[:, 13312:14336]
            KEND = RCb[:, 14336:15360].rearrange("p (k c) -> p k c", k=8)
            ATM = [RCb[:, 15360 + i * 128:15360 + (i + 1) * 128] for i in range(2)]
            dW1, dW2, dW3, dOACC, dQS, dSG, dV, dQDEC, dKINV, dKENDT, dKEND = [Dep() for _ in range(11)]
            dATM = [Dep(), Dep()]
            sdeps = [dW1, dW2, dW3, dOACC, dQS, dSG, dV, dQDEC, dKINV, dKENDT, dKEND] + dATM
            kb.retire(dRC, sdeps)
            hd = [dH(c) for c in range(0, KC, 2)]
            c3 = lambda ap: ap.rearrange("p (c s) -> p c s", s=32)
            for h in range(16):
                cs = slice(h * 128, (h + 1) * 128)
                linear_fm(I["b_wq"][:, cs], KC, 128, hsrc, hd, blk=128,
                          consume=lambda oc, t0, tn, bk: kb.op("act", lambda e: e.activation(out=QS[:, t0:t0 + tn], in_=PS[bk][:, 0:tn], func=AF.Silu), rd=[dPS[bk]], wr=[dQS]))
                linear_fm(I["b_wg"][:, cs], KC, 128, hsrc, hd, blk=128,
                          consume=lambda oc, t0, tn, bk: kb.op("act", lambda e: e.activation(out=SG[:, t0:t0 + tn], in_=PS[bk][:, 0:tn], func=AF.Silu), rd=[dPS[bk]], wr=[dSG]))
                linear_tm(I["b_wi"][:, cs], KC, 128, hsrc, hd, blk=128,
                          consume=lambda tt, c0, cn, bk: copy_op(ev_eng(), VT[:, tt, :], PS[bk][:, 0:128], [dPS[bk]], [dV]))
                for d in range(2):
                    col = d * 16 + h
                    linear_fm(I["b_wf%d" % d][:, cs], KC, 128, hsrc, hd, blk=128,
                              consume=lambda oc, t0, tn, bk: kb.op("act", lambda e: e.activation(out=W1[:, t0:t0 + tn], in_=PS[bk][:, 0:tn], func=AF.Sigmoid), rd=[dPS[bk]], wr=[dW1]))
                    kb.op("dve", lambda e: e.tensor_scalar(out=W1, in0=W1, scalar1=oml[:, col:col + 1], scalar2=lbv[:, col:col + 1], op0=ALU.mult, op1=ALU.add),
                          rd=[dW1, dSMALL], wr=[dW1])
                    kb.op("act", lambda e: e.activation(out=W2, in_=W1, func=AF.Ln), rd=[dW1], wr=[dW2])
                    kb.op("dve", lambda e: e.tensor_scalar(out=W1, in0=W1, scalar1=-1.0, scalar2=1.0, op0=ALU.mult, op1=ALU.add), rd=[dW1, dW2], wr=[dW1])
                    kb.op("dve", lambda e: e.tensor_tensor_scan(out=W3, data0=SCANM[:], data1=W2, initial=0.0, op0=ALU.mult, op1=ALU.add), rd=[dW2, dTRI], wr=[dW3])
                    kb.op("act", lambda e: e.activation(out=EBL[:], in_=c3(W3)[:, :, 31], func=AF.Exp), rd=[dW3], wr=[dEBL])
                    if d == 0:
                        BC, FREE, dBC, dFREE = W3, W2, dW3, dW2
                    else:
                        kb.op("dve", lambda e: e.tensor_tensor(out=W2, in0=W2, in1=W3, op=ALU.subtract), rd=[dW2, dW3], wr=[dW2])
                        kb.op("dve", lambda e: e.tensor_tensor(out=c3(W2), in0=c3(W2), in1=c3(W3)[:, :, 31:32].to_broadcast([128, 32, 32]), op=ALU.add),
                              rd=[dW2, dW3], wr=[dW2])
                        BC, FREE, dBC, dFREE = W2, W3, dW2, dW3
                    kb.op("act", lambda e: e.activation(out=FREE, in_=BC, func=AF.Exp), rd=[dBC, dEBL], wr=[dFREE])
                    kb.op("dve", lambda e: e.tensor_tensor(out=QDEC, in0=QS, in1=FREE, op=ALU.mult), rd=[dQS, dFREE], wr=[dQDEC])
                    kb.op("act", lambda e: e.activation(out=FREE, in_=BC, func=AF.Exp, scale=-1.0), rd=[dBC, dQDEC], wr=[dFREE])
                    kb.op("dve", lambda e: e.tensor_tensor(out=FREE, in0=FREE, in1=W1, op=ALU.mult), rd=[dFREE, dW1], wr=[dFREE])
                    kb.op("act", lambda e: e.activation(out=KINV, in_=FREE, func=AF.Identity, bias=0.0, scale=1.0), rd=[dFREE], wr=[dKINV])
                    kb.op("dve", lambda e: e.tensor_tensor(out=c3(KENDT), in0=c3(FREE), in1=EBL[:].unsqueeze(2).to_broadcast([128, 32, 32]), op=ALU.mult),
                          rd=[dFREE, dEBL], wr=[dKENDT])
                    for b in range(8):
                        bk = bank()
                        pst = PS[bk][:].bitcast(BF16)
                        kb.pe_group([lambda e, b=b, pst=pst: e.transpose(out=pst[:, 0:128], in_=KENDT[:, b * 128:(b + 1) * 128], identity=IDENT[:])],
                                    rd=[dKENDT, dC], wr=[dPS[bk]])
                        copy_op(ev_eng(), KEND[:, b, :], pst[:, 0:128], [dPS[bk]], [dKEND])
                    kb.dma("sp", SST[:], I["sb_f" if d == 0 else "sb_b"][h], wr=[dSST])
                    kb.op("act", lambda e: e.activation(out=SSTb[:], in_=SST[:], func=AF.Identity, bias=0.0, scale=1.0), rd=[dSST], wr=[dSSTb])
                    OUTS = OUT["b_f" if d == 0 else "b_b"]
                    blocks = range(8) if d == 0 else range(7, -1, -1)
                    for b in blocks:
                        ab = bank()
                        kb.pe_group([lambda e, b=b, ab=ab: e.matmul(PS[ab][:, 0:128], lhsT=KINV[:, b * 128:(b + 1) * 128], rhs=QDEC[:, b * 128:(b + 1) * 128], start=True, stop=True)],
                                    rd=[dKINV, dQDEC], wr=[dPS[ab]])
                        atm, datm = ATM[b % 2], dATM[b % 2]
                        kb.op("dve", lambda e, ab=ab, atm=atm: e.tensor_tensor(out=atm, in0=PS[ab][:, 0:128], in1=BMASK[:, d * 128:(d + 1) * 128], op=ALU.mult),
                              rd=[dPS[ab], dC], wr=[datm])
                        ob = bank()
                        kb.op("pe", lambda e, b=b, ob=ob, atm=atm: e.matmul(PS[ob][:, 0:128], lhsT=VT[:, b, :], rhs=atm, start=True, stop=False),
                              rd=[dV, datm], wr=[dPS[ob]])
                        ccs = range(4) if d == 0 else range(3, -1, -1)
                        for ci, cc in enumerate(ccs):
                            c = 4 * b + cc
                            boundary = (d == 0 and c % 8 == 0 and c > 0) or (d == 1 and c % 8 == 7 and c < 31)
                            if boundary:
                                seq = (c // 8 - 1) if d == 0 else (c + 1) // 8
                                kb.dma("sp", OUTS[seq, h], SST[:], rd=[dSST])
                                kb.op("dve", lambda e: e.tensor_scalar_mul(out=SSTb[:], in0=SST[:], scalar1=KEEP), rd=[dSST, dSMALL], wr=[dSSTb])
                                kb.op("dve", lambda e: e.tensor_scalar_mul(out=SST[:], in0=SST[:], scalar1=KEEP), rd=[dSSTb, dSMALL], wr=[dSST])
                            kb.op("pe", lambda e, c=c, cc=cc, ob=ob, ci=ci: e.matmul(PS[ob][:, cc * 32:(cc + 1) * 32], lhsT=SSTb[:], rhs=QDEC[:, c * 32:(c + 1) * 32],
                                                                                      start=False, stop=(ci == 3), skip_group_check=True),
                                  rd=[dSSTb, dQDEC], wr=[dPS[ob]] if ci == 3 else [])
                            ub = bank()
                            while ub == ob:
                                ub = bank()
                            kb.pe_group([lambda e, b=b, cc=cc, ub=ub: e.matmul(PS[ub][:, 0:128], lhsT=KEND[cc * 32:(cc + 1) * 32, b, :], rhs=VT[cc * 32:(cc + 1) * 32, b, :],
                                                                               start=True, stop=True, tile_position=(cc * 32, 0))],
                                        rd=[dKEND, dV], wr=[dPS[ub]])
                            kb.op("dve", lambda e, c=c, ub=ub: e.scalar_tensor_tensor(out=SSTb[:], in0=SST[:], scalar=EBL[:, c:c + 1], in1=PS[ub][:, 0:128], op0=ALU.mult, op1=ALU.add),
                                  rd=[dSST, dEBL, dPS[ub]], wr=[dSSTb])
                            kb.op("dve", lambda e, c=c, ub=ub: e.scalar_tensor_tensor(out=SST[:], in0=SST[:], scalar=EBL[:, c:c + 1], in1=PS[ub][:, 0:128], op0=ALU.mult, op1=ALU.add),
                                  rd=[dEBL, dPS[ub]], wr=[dSST])
                        if d == 0:
                            copy_op("act", OACC[:, b * 128:(b + 1) * 128], PS[ob][:, 0:128], [dPS[ob]], [dOACC])
                        else:
                            kb.op("dve", lambda e, b=b, ob=ob: e.tensor_tensor(out=OACC[:, b * 128:(b + 1) * 128], in0=OACC[:, b * 128:(b + 1) * 128], in1=PS[ob][:, 0:128], op=ALU.add),
                                  rd=[dPS[ob]], wr=[dOACC])
                    kb.dma("sp", OUTS[3 if d == 0 else 0, h], SST[:], rd=[dSST])
                for t0 in (0, 512):
                    rstd_half(lambda c, a, n: OACC[:, a:a + n], lambda c: dOACC, 1, 128, t0)
                    tmp = MS[:, 0, :]
                    kb.op("dve", lambda e: e.tensor_tensor(out=tmp, in0=OACC[:, t0:t0 + 512], in1=RSTD[:], op=ALU.mult), rd=[dOACC, dRSTD], wr=[dMS[0]])
                    kb.op("dve", lambda e: e.scalar_tensor_tensor(out=Ov[:, h, t0:t0 + 512], in0=tmp, scalar=GN, in1=SG[:, t0:t0 + 512], op0=ALU.mult, op1=ALU.mult),
                          rd=[dMS[0], dSG, dSMALL], wr=[dO(h)])
            out_proj(I["b_wo"], sdeps)

        EBLB = sb("EBLB", [128, 64])

        def ffn(l):
            norm_mod(GSF, SHF)
            HID = [MS[:, 0:2, :].rearrange("p a b -> p (a b)").bitcast(BF16).rearrange("p (j t) -> p j t", j=2),
                   MS[:, 2:4, :].rearrange("p a b -> p (a b)").bitcast(BF16).rearrange("p (j t) -> p j t", j=2)]
            dHID = [[dMS[0], dMS[1]], [dMS[2], dMS[3]]]
            SGv = [RSTD[:].bitcast(BF16)[:, 0:512], RSTD[:].bitcast(BF16)[:, 512:1024], TRI[:, 0:512], TRI[:, 512:1024]]
            dSG = [dRSTD, dRSTD, dTRI, dTRI]
            hd = [dH(c) for c in range(0, KC, 2)]
            nsl = DFF // 256
            for s in range(nsl):
                hid, dhid = HID[s % 2], dHID[s % 2]
                tg, dtg, wid = ws.next(I["ffn_wg"][l][:, s * 256:(s + 1) * 256], KC, 256)
                for idx, (j, t0) in enumerate(((0, 0), (0, 512), (1, 0), (1, 512))):
                    bg = bank()
                    kb.pe_group([(lambda e, k=k, bg=bg: e.matmul(PS[bg][:, :], lhsT=tg[:, k, j * 128:(j + 1) * 128], rhs=Hv[:, k, t0:t0 + 512],
                                                                  start=(k == 0), stop=(k == KC - 1))) for k in range(KC)], rd=[dtg] + hd, wr=[dPS[bg]])
                    if idx == 3:
                        ws.release(wid)
                    kb.op("act", lambda e, bg=bg, idx=idx: e.activation(out=SGv[idx], in_=PS[bg][:, :], func=AF.Silu), rd=[dPS[bg]], wr=[dSG[idx]])
                tu, dtu, wid = ws.next(I["ffn_wu"][l][:, s * 256:(s + 1) * 256], KC, 256)
                for idx, (j, t0) in enumerate(((0, 0), (0, 512), (1, 0), (1, 512))):
                    bu = bank()
                    kb.pe_group([(lambda e, k=k, bu=bu: e.matmul(PS[bu][:, :], lhsT=tu[:, k, j * 128:(j + 1) * 128], rhs=Hv[:, k, t0:t0 + 512],
                                                                  start=(k == 0), stop=(k == KC - 1))) for k in range(KC)], rd=[dtu] + hd, wr=[dPS[bu]])
                    if idx == 3:
                        ws.release(wid)
                    kb.op("dve", lambda e, bu=bu, j=j, t0=t0, idx=idx: e.tensor_tensor(out=hid[:, j, t0:t0 + 512], in0=PS[bu][:, :], in1=SGv[idx], op=ALU.mult),
                          rd=[dPS[bu], dSG[idx]], wr=dhid)
                td, dtd, wid = ws.next(I["ffn_wd"][l][s * 256:(s + 1) * 256, :], 2, DM)
                for oc in range(KC):
                    for t0 in (0, 512):
                        bk = bank()
                        kb.pe_group([(lambda e, k=k, bk=bk: e.matmul(PS[bk][:, :], lhsT=td[:, k, oc * 128:(oc + 1) * 128], rhs=hid[:, k, t0:t0 + 512],
                                                                      start=(k == 0), stop=(k == 1))) for k in range(2)], rd=[dtd] + dhid, wr=[dPS[bk]])
                        if oc == KC - 1 and t0 == 512:
                            ws.release(wid)
                        if s == 0:
                            copy_op(ev_eng(), yffn(oc, t0, 512), PS[bk][:, :], [dPS[bk]], [yffn_dep(oc)])
                        else:
                            kb.op("dve", lambda e, bk=bk, oc=oc, t0=t0: e.tensor_tensor(out=yffn(oc, t0, 512), in0=yffn(oc, t0, 512), in1=PS[bk][:, :], op=ALU.add),
                                  rd=[dPS[bk]], wr=[yffn_dep(oc)])
            post_norm_residual(yffn, yffn_dep, GGF)

        def emit():
            kb.op("dve", lambda e: e.memset(EPSB[:], EPS), wr=[dC])
            load_consts()
            for l in range(n_layers):
                if "ada" not in skip:
                    ada(l)
                if "norm" not in skip:
                    norm_mod(GSM, SHM)
                m = l % 4
                if ("mixer%d" % m) in skip:
                    pass
                elif m == 0:
                    mixer_a(l)
                elif m == 1:
                    mixer_b(l)
                elif m == 2:
                    mixer_c()
                else:
                    mixer_d()
                if "ffn" not in skip:
                    ffn(l)
            for c in range(KC):
                kb.dma("sp", OUT["yT"][c * 128:(c + 1) * 128, :], X[:, c, :], rd=[dX[c]])
            if not kb.dry:
                for qn in list(kb.dma_ring.keys()):
                    for j, slot in enumerate(kb.dma_ring[qn]):
                        kb._wait(kb.E["sp"], (slot[0], 16 * slot[1], ("dma", qn, j)))

        kb.dry = True
        emit()
        kb.dry = False
        rr[0] = 0
        evq[0] = 0
        emit()
        nc._used_inputs = list(I.keys())
        print("[kernel] inst counts", {k: v.n_inst for k, v in kb.E.items()}, "nsem", kb.nsem, "wblocks", len(ws.req), flush=True)
    return nc


def _colT(v):
    v = np.asarray(v, np.float32)
    return np.ascontiguousarray(v.reshape(-1, 128).T)


def _rope_tables(identity):
    C = np.ones((128, T), np.float32)
    S = np.zeros((128, T), np.float32)
    if not identity:
        t = np.arange(T)
        row, col = (t // 64).astype(np.float32), (t % 64).astype(np.float32)
        inv = (np.float32(10000.0) ** (-np.arange(16, dtype=np.float32) / np.float32(16))).astype(np.float32)
        for p in range(128):
            q = p % 64
            pos = row if q < 32 else col
            ang = (pos * inv[q % 16]).astype(np.float32)
            C[p] = np.cos(ang)
            S[p] = -np.sin(ang) if (q % 32) < 16 else np.sin(ang)
    return C, S


def _consts(is_ctx):
    c = {}
    c["ident"] = np.eye(128, dtype=np.float32)
    perm = np.zeros((128, 128), np.float32)
    for m in range(128):
        k = m + 16 if (m % 32) < 16 else m - 16
        perm[k, m] = 1.0
    c["perm"] = perm
    mq = np.zeros((8, T), np.float32)
    mk = np.zeros((8, NK), np.float32)
    keys = np.arange(NK)
    for j in range(4):
        mk[j] = ((keys >= NPAST) & ((keys - NPAST) // 256 == j)).astype(np.float32)
    mk[4] = (keys < NPAST).astype(np.float32)
    tri = np.zeros((128, 2, 4, 128), np.float32)
    if is_ctx:
        q = np.arange(T)
        for j in range(4):
            mq[j] = np.where(q // 256 == j, 0.0, NEGBIG)
        mq[4] = NEGBIG
    else:
        kk = np.arange(128)[:, None]
        qq = np.arange(128)[None, :]
        tri[:, 0] = np.where(kk >= qq, 0.0, NEGBIG)[:, None, :]
        tri[:, 1] = np.where(kk <= qq, 0.0, NEGBIG)[:, None, :]
    c["maskq"], c["maskk"], c["tri"] = mq, mk, tri.reshape(128, 1024)
    db = np.zeros((128, 40), np.float32)
    if is_ctx:
        for qb in range(8):
            kts = [None, None] + [kb_ for kb_ in (qb - 1, qb, qb + 1) if 0 <= kb_ < 8]
            for i, kb_ in enumerate(kts):
                if kb_ is None or (kb_ // 2) != (qb // 2):
                    db[:, qb * 5 + i] = NEGBIG
    c["dbias"] = db
    c["ropeC"], c["ropeS"] = _rope_tables(is_ctx)
    s = np.arange(128)[:, None]
    t = np.arange(128)[None, :]
    same = (s // 32) == (t // 32)
    bm = np.zeros((128, 256), np.float32)
    bm[:, 0:128] = (same & (s <= t)).astype(np.float32)
    bm[:, 128:256] = (same & (s >= t)).astype(np.float32)
    c["bmask"] = bm
    sm = np.ones((128, T), np.float32)
    sm[:, ::32] = 0.0
    c["scanm"] = sm
    return c


_PROGRAM_CACHE = {}
BUILD_ARGS = {}


def kernel(**inp):
    f = lambda k: np.asarray(inp[k], np.float32)
    shared = {}
    shared["ada_w"] = f("ada_w")
    shared["ada_bT"] = np.ascontiguousarray(np.concatenate([_colT(f("ada_b")[l]) for l in range(4)], axis=1))
    shared["gains"] = np.ascontiguousarray(np.concatenate(
        [_colT(f(n)[l]) for n in ("norm_mix_pre", "norm_mix_post", "norm_ffn_pre", "norm_ffn_post") for l in range(4)], axis=1))
    for n in ("a_wq", "a_wk", "a_wv", "a_wo", "b_wq", "b_wi", "b_wg", "b_wo", "c_wdq", "c_wuk", "c_wuv", "c_wo", "d_wq", "d_wv", "d_wo"):
        shared[n] = f(n)[0]
    shared["b_wf0"], shared["b_wf1"] = f("b_wf")[0, 0], f("b_wf")[0, 1]
    shared["a_lam"] = np.ascontiguousarray(np.broadcast_to(f("a_lambda")[0].reshape(1, 256), (128, 256)))
    shared["a_subln"] = f("a_subln")[0].reshape(128, 1)
    shared["b_lowerT"] = np.ascontiguousarray(np.concatenate([_colT(f("b_lower")[l, d]) for l in range(4) for d in range(2)], axis=1))
    shared["b_gnorm"] = f("b_gnorm")[0].reshape(128, 1)
    shared["c_qnormT"] = _colT(f("c_qnorm")[0])
    shared["c_kvnormT"] = _colT(f("c_kvnorm")[0])
    wuq = f("c_wuq")[0]
    pn = [h * 192 + d for h in range(16) for d in range(128)] + [h * 192 + 128 + d for h in range(16) for d in range(64)]
    shared["c_wuq"] = np.ascontiguousarray(wuq[:, pn])
    wdkv = f("c_wdkv")[0]
    shared["c_wdkv"] = np.ascontiguousarray(np.concatenate([wdkv[:, 0:256], wdkv[:, 256:320], wdkv[:, 256:320]], axis=1))
    wk = f("d_wk")[0]
    shared["d_wk"] = np.ascontiguousarray(np.concatenate([wk[:, g * 64:(g + 1) * 64] for g in range(4) for _ in range(2)], axis=1))
    sink = f("d_sink")[0]
    sp = np.asarray(sink, np.float32)
    shared["d_sinkP"] = np.ascontiguousarray(np.broadcast_to(sp.reshape(1, 32), (128, 32)))
    shared["ffn_wg"], shared["ffn_wu"], shared["ffn_wd"] = f("ffn_wg"), f("ffn_wu"), f("ffn_wd")
    cc = {True: _consts(True), False: _consts(False)}

    in_maps = []
    xp, xs = f("x_prompt"), f("x_sample")
    for core in range(8):
        m = dict(shared)
        is_ctx = core < 4
        m.update(cc[is_ctx])
        if is_ctx:
            m["xT"] = np.ascontiguousarray(xp[4 * core:4 * core + 4].reshape(T, DM).T)
            m["condT"] = _colT(f("c_ctx"))
            m["pa_kT"] = np.zeros((DM, NPAST), np.float32)
            m["pa_v"] = np.zeros((NPAST, DM), np.float32)
            m["sb_f"] = np.zeros((16, 128, 128), np.float32)
            m["sb_b"] = np.zeros((16, 128, 128), np.float32)
            m["keep"] = np.zeros((128, 1), np.float32)
            m["pc_ckvT"] = np.zeros((256, NPAST), np.float32)
            m["pc_kpeT2"] = np.zeros((128, NPAST), np.float32)
            m["pd_kT2"] = np.zeros((512, NPAST), np.float32)
            m["pd_v"] = np.zeros((NPAST, 256), np.float32)
        else:
            b = core - 4
            m["xT"] = np.ascontiguousarray(xs[b].T)
            m["condT"] = _colT(f("c")[b])
            m["pa_kT"] = np.ascontiguousarray(f("cache_a_k")[b, 0].reshape(NPAST, DM).T)
            m["pa_v"] = np.ascontiguousarray(f("cache_a_v")[b, 0].reshape(NPAST, DM))
            m["sb_f"] = np.ascontiguousarray(f("state_b_fwd")[b, 0])
            m["sb_b"] = np.ascontiguousarray(f("state_b_bwd")[b, 0])
            m["keep"] = np.ones((128, 1), np.float32)
            m["pc_ckvT"] = np.ascontiguousarray(f("cache_c_ckv")[b, 0].T)
            kpeT = f("cache_c_kpe")[b, 0].T
            m["pc_kpeT2"] = np.ascontiguousarray(np.concatenate([kpeT, kpeT], axis=0))
            dk = f("cache_d_k")[b, 0]
            m["pd_kT2"] = np.ascontiguousarray(np.concatenate([dk[:, g, :].T for g in range(4) for _ in range(2)], axis=0))
            m["pd_v"] = np.ascontiguousarray(f("cache_d_v")[b, 0].reshape(NPAST, 256))
        in_maps.append(m)

    if "nc" not in _PROGRAM_CACHE:
        _PROGRAM_CACHE["nc"] = build_program(**BUILD_ARGS)
    nc = _PROGRAM_CACHE["nc"]
    used = set(nc._used_inputs)
    in_maps = [{k: v for k, v in m.items() if k in used} for m in in_maps]
    res = run_bass_kernel_spmd(nc, in_maps, core_ids=list(range(8)))
    R = res.results

    y_prompt = np.concatenate([R[k]["yT"].T.reshape(4, 256, DM) for k in range(4)], axis=0)
    y_sample = np.stack([R[4 + b]["yT"].T for b in range(4)], axis=0)
    a_k = np.concatenate([R[k]["a_kT"].T.reshape(4, 1, 256, 16, 128) for k in range(4)], axis=0)
    a_v = np.concatenate([R[k]["a_v"].reshape(4, 1, 256, 16, 128) for k in range(4)], axis=0)
    b_f = np.concatenate([R[k]["b_f"].reshape(4, 1, 16, 128, 128) for k in range(4)], axis=0)
    b_b = np.concatenate([R[k]["b_b"].reshape(4, 1, 16, 128, 128) for k in range(4)], axis=0)
    c_ckv = np.concatenate([R[k]["c_ckvT"].T.reshape(4, 1, 256, 256) for k in range(4)], axis=0)
    c_kpe = np.concatenate([R[k]["c_kpeT"].T.reshape(4, 1, 256, 64) for k in range(4)], axis=0)
    d_k = np.concatenate([R[k]["d_kT"].T.reshape(4, 1, 256, 4, 64) for k in range(4)], axis=0)
    d_v = np.concatenate([R[k]["d_v"].reshape(4, 1, 256, 4, 64) for k in range(4)], axis=0)
    outs = (y_prompt, y_sample, a_k, a_v, b_f, b_b, c_ckv, c_kpe, d_k, d_v)
    return tuple(np.ascontiguousarray(o, dtype=np.float32) for o in outs)
```

```python
import math
import numpy as np
from contextlib import ExitStack
import concourse.bass as bass
import concourse.mybir as mybir
from concourse.bass_utils import run_bass_kernel_spmd

F32 = mybir.dt.float32
BF16 = mybir.dt.bfloat16
AF = mybir.ActivationFunctionType
ALU = mybir.AluOpType
AX = mybir.AxisListType

EPOCH = 4096
T = 1024
NPAST = 256
NK = 1280
DM = 2048
KC = 16
DFF = 5632
EPS = 1e-6
NEGBIG = -30000.0


class Dep:
    __slots__ = ("w", "r", "excl")

    def __init__(self, excl=False):
        self.w = None
        self.r = []
        self.excl = excl


class Eng:
    def __init__(self, kb, name, h):
        self.kb, self.name, self.h = kb, name, h
        self.sems = []
        self.count = 0
        self.waited = {}
        self.n_inst = 0

    def cur_sem(self):
        if not self.sems or self.count >= EPOCH:
            self.sems.append(self.kb.new_sem(f"{self.name}{len(self.sems)}"))
            self.count = 0
        return self.sems[-1]


class KB:
    def __init__(self, nc, stack):
        self.nc = nc
        self.stack = stack
        self.nsem = 0
        self.dry = False
        self.E = {
            "pe": Eng(self, "pe", nc.tensor),
            "act": Eng(self, "act", nc.scalar),
            "dve": Eng(self, "dve", nc.vector),
            "pool": Eng(self, "pool", nc.gpsimd),
            "sp": Eng(self, "sp", nc.sync),
        }
        self.dma_ring = {}
        self.dma_i = {}
        self.NR = 12
        self.n_mm = 0
        self.marks = []

    def new_sem(self, name):
        self.nsem += 1
        return self.stack.enter_context(self.nc.semaphore(f"s_{name}_{self.nsem}"))

    def _wait(self, e, tok):
        if tok is None:
            return
        sem, val, key = tok
        if key[0] == "pe" and e.name == "pe":
            return
        if e.waited.get(key, 0) >= val:
            return
        e.h.wait_ge(sem, val)
        e.n_inst += 1
        e.waited[key] = val

    def _pre(self, e, rd, wr):
        for d in rd:
            self._wait(e, d.w)
            if d.excl:
                for t in d.r:
                    if t[2][0] != e.name:
                        self._wait(e, t)
        for d in wr:
            self._wait(e, d.w)
            for t in d.r:
                self._wait(e, t)

    def _post(self, tok, rd, wr):
        for d in rd:
            d.r.append(tok)
            if len(d.r) > 16:
                best = {}
                for t in d.r:
                    if t[2] not in best or best[t[2]][1] < t[1]:
                        best[t[2]] = t
                d.r = list(best.values())
        for d in wr:
            d.w = tok
            d.r = []

    def op(self, en, fn, rd=(), wr=()):
        if self.dry:
            return None
        e = self.E[en]
        self._pre(e, rd, wr)
        sem = e.cur_sem()
        inst = fn(e.h)
        inst.then_inc(sem, 1)
        if en == "pe":
            self.n_mm += 1
        e.count += 1
        e.n_inst += 1
        tok = (sem, e.count, (en, len(e.sems)))
        self._post(tok, rd, wr)
        return tok

    def pe_group(self, fns, rd=(), wr=()):
        if self.dry:
            return None
        e = self.E["pe"]
        self._pre(e, rd, wr)
        sem = e.cur_sem()
        inst = None
        for fn in fns:
            inst = fn(e.h)
            e.n_inst += 1
        self.n_mm += len(fns)
        inst.then_inc(sem, 1)
        e.count += 1
        tok = (sem, e.count, ("pe", len(e.sems)))
        self._post(tok, rd, wr)
        return tok

    def dma(self, qn, out, in_, rd=(), wr=()):
        if self.dry:
            return None
        e = self.E[qn]
        ring = self.dma_ring.setdefault(qn, [])
        i = self.dma_i.get(qn, 0)
        self.dma_i[qn] = i + 1
        j = i % self.NR
        if j >= len(ring):
            ring.append([self.new_sem(f"dma_{qn}{j}"), 0])
        slot = ring[j]
        key = ("dma", qn, j)
        if slot[1] > 0:
            self._wait(e, (slot[0], 16 * slot[1], key))
        self._pre(e, rd, wr)
        inst = e.h.dma_start(out=out, in_=in_)
        slot[1] += 1
        inst.then_inc(slot[0], 16)
        e.n_inst += 1
        tok = (slot[0], 16 * slot[1], key)
        self._post(tok, rd, wr)
        return tok

    def retire(self, olds, news):
        toks = []
        for d in olds:
            if d.w is not None:
                toks.append(d.w)
            toks.extend(d.r)
        for d in news:
            d.r.extend(toks)


IN_SPECS = [
    ("xT", (DM, T)), ("condT", (128, 16)), ("ada_w", (4, DM, 6 * DM)), ("ada_bT", (128, 384)),
    ("gains", (128, 256)),
    ("a_wq", (DM, DM)), ("a_wk", (DM, DM)), ("a_wv", (DM, DM)), ("a_wo", (DM, DM)),
    ("a_lam", (128, 256)), ("a_subln", (128, 1)), ("pa_kT", (DM, NPAST)), ("pa_v", (NPAST, DM)),
    ("b_wq", (DM, DM)), ("b_wi", (DM, DM)), ("b_wf0", (DM, DM)), ("b_wf1", (DM, DM)), ("b_wg", (DM, DM)),
    ("b_wo", (DM, DM)), ("b_lowerT", (128, 128)), ("b_gnorm", (128, 1)), ("sb_f", (16, 128, 128)),
    ("sb_b", (16, 128, 128)), ("keep", (128, 1)),
    ("c_wdq", (DM, 512)), ("c_qnormT", (128, 4)), ("c_wuq", (512, 3072)), ("c_wdkv", (DM, 384)),
    ("c_kvnormT", (128, 2)), ("c_wuk", (256, DM)), ("c_wuv", (256, DM)), ("c_wo", (DM, DM)),
    ("pc_ckvT", (256, NPAST)), ("pc_kpeT2", (128, NPAST)),
    ("d_wq", (DM, DM)), ("d_wk", (DM, 512)), ("d_wv", (DM, 256)), ("d_wo", (DM, DM)), ("d_sinkP", (128, 32)),
    ("pd_kT2", (512, NPAST)), ("pd_v", (NPAST, 256)),
    ("ffn_wg", (4, DM, DFF)), ("ffn_wu", (4, DM, DFF)), ("ffn_wd", (4, DFF, DM)),
    ("ident", (128, 128)), ("perm", (128, 128)), ("maskq", (8, T)), ("maskk", (8, NK)), ("tri", (128, 1024)),
    ("ropeC", (128, T)), ("ropeS", (128, T)), ("bmask", (128, 256)), ("scanm", (128, T)), ("dbias", (128, 40)), ("abias", (128, 40)),
]
OUT_SPECS = [
    ("yT", (DM, T)), ("a_kT", (DM, T)), ("a_v", (T, DM)), ("b_f", (4, 16, 128, 128)), ("b_b", (4, 16, 128, 128)),
    ("c_ckvT", (256, T)), ("c_kpeT", (64, T)), ("d_kT", (256, T)), ("d_v", (T, 256)),
]


def build_program(n_layers=4, skip=()):
    nc = bass.Bass("TRN2", target_bir_lowering=False)
    class _LazyIn(dict):
        def __missing__(self, n):
            self[n] = nc.dram_tensor(n, list(dict(IN_SPECS)[n]), F32, kind="ExternalInput").ap()
            return self[n]
    I = _LazyIn()
    OUT = {n: nc.dram_tensor(n, list(s), F32, kind="ExternalOutput").ap() for n, s in OUT_SPECS}
    with ExitStack() as st:
        kb = KB(nc, st)

        def sb(name, shape, dt=F32):
            return st.enter_context(nc.sbuf_tensor(name, list(shape), dt))

        X = sb("X", [128, KC, T])
        dX = [Dep() for _ in range(KC)]
        RA, RB, RC = sb("RA", [128, 8192]), sb("RB", [128, 8192]), sb("RC", [128, 8192])
        dRA, dRB, dRC = [Dep() for _ in range(8)], [Dep() for _ in range(8)], [Dep() for _ in range(8)]
        RAb, RBb, RCb = RA[:].bitcast(BF16), RB[:].bitcast(BF16), RC[:].bitcast(BF16)
        Hv = RAb.rearrange("p (c t) -> p c t", c=KC)
        Ov = RBb.rearrange("p (c t) -> p c t", c=KC)
        dH = lambda c: dRA[c // 2]
        dO = lambda c: dRB[c // 2]
        RAf = RA[:].rearrange("p (c t) -> p c t", c=8)
        RBf = RB[:].rearrange("p (c t) -> p c t", c=8)
        RCf = RC[:].rearrange("p (c t) -> p c t", c=8)

        WB = [sb(f"WB{i}", [128, 4096], BF16) for i in range(2)]
        dWB = [Dep() for _ in range(2)]
        MS = sb("MS", [128, 4, 512])
        dMS = [Dep() for _ in range(4)]
        XBt = sb("XBt", [128, 512], BF16)
        dXB = Dep()
        RSTD = sb("RSTD", [128, 512])
        dRSTD = Dep()
        ROPEC, ROPES = sb("ROPEC", [128, T]), sb("ROPES", [128, T])
        ONES = sb("ONES", [128, 128], BF16)
        IDENT = sb("IDENT", [128, 128], BF16)
        PERM = sb("PERM", [128, 128], BF16)
        MQ = sb("MQ", [8, T], BF16)
        MK = sb("MK", [8, NK], BF16)
        TRI = sb("TRI", [128, 1024], BF16)
        BMASK = sb("BMASK", [128, 256], BF16)
        SCANM = TRI
        dTRI = Dep()
        GAINS = sb("GAINS", [128, 256])
        ADAB = sb("ADAB", [128, 96])
        MODT = sb("MODT", [128, 96])
        GSM, SHM, GGM = sb("GSM", [128, 16]), sb("SHM", [128, 16]), sb("GGM", [128, 16])
        GSF, SHF, GGF = sb("GSF", [128, 16]), sb("SHF", [128, 16]), sb("GGF", [128, 16])
        SC = sb("SC", [128, 16], BF16)
        SMALL = sb("SMALL", [128, 64])
        ESINK = sb("ESINK", [128, 32])
        DBIAS = sb("DBIAS", [128, 40])
        ABIAS = sb("ABIAS", [128, 40])
        LBT = sb("LBT", [128, 128])
        EBL = sb("EBL", [128, 32])
        SST = sb("SST", [128, 128])
        SSTb = sb("SSTb", [128, 128], BF16)
        SSTb2 = sb("SSTb2", [128, 128], BF16)
        dSSTb2 = Dep()
        dC = Dep()
        dMOD = Dep()
        dSMALL = Dep()
        dEBL = Dep()
        dSST = Dep()
        dSSTb = Dep()

        PS = [st.enter_context(nc.psum_tensor(f"PS{i}", [128, 512], F32)) for i in range(8)]
        dPS = [Dep(excl=True) for _ in range(8)]
        rr = [0]

        def bank():
            rr[0] = (rr[0] + 1) % 8
            return rr[0]

        evq = [0]

        def ev_eng():
            evq[0] ^= 1
            return "act" if evq[0] else "dve"

        def copy_op(en, out, in_, rd, wr):
            if en == "act":
                return kb.op("act", lambda e: e.activation(out=out, in_=in_, func=AF.Identity, bias=0.0, scale=1.0), rd=rd, wr=wr)
            return kb.op("dve", lambda e: e.tensor_copy(out=out, in_=in_), rd=rd, wr=wr)

        class WS:
            def __init__(self):
                self.req = []
                self.i = 0
                self.issued = 0
                self.released = set()

            def _issue(self, j):
                view, kc, ncol = self.req[j]
                buf = WB[j % 2][:, 0:kc * ncol].rearrange("p (k n) -> p k n", k=kc)
                kb.dma("pool", buf, view.rearrange("(k p) n -> p k n", p=128), wr=[dWB[j % 2]])

            def _pump(self):
                while (self.issued < len(self.req) and self.issued < self.i + 2
                       and (self.issued < 2 or (self.issued - 2) in self.released)):
                    self._issue(self.issued)
                    self.issued += 1

            def next(self, view, kc, ncol):
                if kb.dry:
                    self.req.append((view, kc, ncol))
                    return WB[0][:, 0:kc * ncol].rearrange("p (k n) -> p k n", k=kc), dWB[0], -1
                j = self.i
                self.i += 1
                self._pump()
                assert self.issued > j, "weight stream: more than 2 blocks alive"
                _, kc2, ncol2 = self.req[j]
                assert (kc2, ncol2) == (kc, ncol)
                return WB[j % 2][:, 0:kc * ncol].rearrange("p (k n) -> p k n", k=kc), dWB[j % 2], j

            def release(self, j):
                if kb.dry:
                    return
                self.released.add(j)
                self._pump()

        ws = WS()

        def linear_fm(wview, kc_n, ncols, src, src_deps, consume, ntok=T, blk=256):
            for b0 in range(0, ncols, blk):
                bn = min(blk, ncols - b0)
                tile, dep, wid = ws.next(wview[:, b0:b0 + bn], kc_n, bn)
                pend = []
                for j in range(bn // 128):
                    oc = (b0 // 128) + j
                    for t0 in range(0, ntok, 512):
                        tn = min(512, ntok - t0)
                        bk = bank()
                        fns = [(lambda e, k=k, bk=bk, j=j, t0=t0, tn=tn: e.matmul(
                            PS[bk][:, 0:tn], lhsT=tile[:, k, j * 128:(j + 1) * 128], rhs=src(k, t0, tn),
                            start=(k == 0), stop=(k == kc_n - 1))) for k in range(kc_n)]
                        kb.pe_group(fns, rd=[dep] + src_deps, wr=[dPS[bk]])
                        if j == bn // 128 - 1 and t0 + 512 >= ntok:
                            ws.release(wid)
                        consume(oc, t0, tn, bk)

        def linear_tm(wview, kc_n, ncols, src, src_deps, consume, ntiles=8, blk=256):
            for b0 in range(0, ncols, blk):
                bn = min(blk, ncols - b0)
                tile, dep, wid = ws.next(wview[:, b0:b0 + bn], kc_n, bn)
                for tt in range(ntiles):
                    bk = bank()
                    fns = [(lambda e, k=k, bk=bk, tt=tt: e.matmul(
                        PS[bk][:, 0:bn], lhsT=src(k, tt * 128, 128), rhs=tile[:, k, :],
                        start=(k == 0), stop=(k == kc_n - 1))) for k in range(kc_n)]
                    kb.pe_group(fns, rd=[dep] + src_deps, wr=[dPS[bk]])
                    if tt == ntiles - 1:
                        ws.release(wid)
                    consume(tt, b0, bn, bk)

        hsrc = lambda k, t0, tn: Hv[:, k, t0:t0 + tn]
        osrc = lambda k, t0, tn: Ov[:, k, t0:t0 + tn]

        def load_consts():
            kb.dma("sp", ROPEC[:], I["ropeC"], wr=[dC])
            kb.dma("sp", ROPES[:], I["ropeS"], wr=[dC])
            kb.dma("pool", BMASK[:], I["bmask"], wr=[dC])
            kb.dma("sp", GAINS[:], I["gains"], wr=[dC])
            kb.dma("sp", ESINK[:], I["d_sinkP"], wr=[dC])
            kb.dma("sp", DBIAS[:], I["dbias"], wr=[dC])
            kb.dma("sp", ABIAS[:], I["abias"], wr=[dC])
            kb.dma("sp", LBT[:], I["b_lowerT"], wr=[dC])
            kb.dma("pool", IDENT[:], I["ident"], wr=[dC])
            kb.dma("pool", PERM[:], I["perm"], wr=[dC])
            kb.dma("pool", MQ[:], I["maskq"], wr=[dC])
            kb.dma("pool", MK[:], I["maskk"], wr=[dC])
            kb.op("dve", lambda e: e.memset(ONES[:], 1.0), wr=[dC])
            kb.dma("sp", SMALL[:, 0:16], I["condT"], wr=[dSMALL])
            kb.op("act", lambda e: e.activation(out=SC[:], in_=SMALL[:, 0:16], func=AF.Silu), rd=[dSMALL], wr=[dC])
            kb.op("act", lambda e: e.activation(out=ESINK[:], in_=ESINK[:], func=AF.Exp), rd=[dC], wr=[dC])
            for c in range(KC):
                kb.dma("sp", X[:, c, :], I["xT"][c * 128:(c + 1) * 128, :], wr=[dX[c]])

        def ada(l):
            kb.dma("sp", ADAB[:], I["ada_bT"][:, l * 96:(l + 1) * 96], wr=[dMOD])
            SCB = MS[:, 0:2, :].rearrange("p a b -> p (a b)").bitcast(BF16).rearrange("p (k m) -> p k m", k=KC)
            kb.op("dve", lambda e: e.tensor_copy(out=SCB, in_=SC[:, :].unsqueeze(2).to_broadcast([128, KC, 128])),
                  rd=[dC], wr=[dMS[0], dMS[1]])
            wv = I["ada_w"][l]
            tmp = MS[:, 2, 0:256].rearrange("p (j q) -> p j q", j=2)
            for b in range(48):
                tile, dep, wid = ws.next(wv[:, b * 256:(b + 1) * 256], KC, 256)
                bk = bank()
                fns = [(lambda e, k=k, bk=bk: e.matmul(PS[bk][:, 0:256], lhsT=SCB[:, k, :], rhs=tile[:, k, :],
                                                       start=(k == 0), stop=(k == KC - 1))) for k in range(KC)]
                kb.pe_group(fns, rd=[dep, dMS[0], dMS[1]], wr=[dPS[bk]])
                ws.release(wid)
                kb.op("dve", lambda e, bk=bk: e.tensor_tensor(out=tmp, in0=PS[bk][:, 0:256].rearrange("p (j q) -> p j q", j=2),
                                                              in1=IDENT[:, :].unsqueeze(1).to_broadcast([128, 2, 128]), op=ALU.mult),
                      rd=[dPS[bk], dC], wr=[dMS[2]])
                kb.op("dve", lambda e, b=b: e.tensor_reduce(out=MODT[:, 2 * b:2 * b + 2], in_=tmp, axis=AX.X, op=ALU.add),
                      rd=[dMS[2]], wr=[dMOD])
            kb.op("dve", lambda e: e.tensor_tensor(out=MODT[:], in0=MODT[:], in1=ADAB[:], op=ALU.add),
                  rd=[dMOD], wr=[dMOD])
            g = lambda n: GAINS[:, n * 64 + l * 16:n * 64 + l * 16 + 16]
            kb.op("dve", lambda e: e.scalar_tensor_tensor(out=GSM[:], in0=MODT[:, 16:32], scalar=1.0, in1=g(0), op0=ALU.add, op1=ALU.mult), rd=[dMOD, dC], wr=[dMOD])
            kb.op("dve", lambda e: e.tensor_copy(out=SHM[:], in_=MODT[:, 0:16]), rd=[dMOD], wr=[dMOD])
            kb.op("dve", lambda e: e.tensor_tensor(out=GGM[:], in0=MODT[:, 32:48], in1=g(1), op=ALU.mult), rd=[dMOD, dC], wr=[dMOD])
            kb.op("dve", lambda e: e.scalar_tensor_tensor(out=GSF[:], in0=MODT[:, 64:80], scalar=1.0, in1=g(2), op0=ALU.add, op1=ALU.mult), rd=[dMOD, dC], wr=[dMOD])
            kb.op("dve", lambda e: e.tensor_copy(out=SHF[:], in_=MODT[:, 48:64]), rd=[dMOD], wr=[dMOD])
            kb.op("dve", lambda e: e.tensor_tensor(out=GGF[:], in0=MODT[:, 80:96], in1=g(3), op=ALU.mult), rd=[dMOD, dC], wr=[dMOD])

        def sq_buf(i):
            return MS[:, 2 + (i % 2), :].bitcast(BF16)[:, 0:512], dMS[2 + (i % 2)]

        def rstd_half(src, sdeps, nch, dtot, t0, bk=None):
            bk = bank() if bk is None else bk
            for c in range(nch):
                sq, dsq = sq_buf(c)
                kb.op("act", lambda e, c=c, sq=sq: e.activation(out=sq, in_=src(c, t0, 512), func=AF.Square),
                      rd=[sdeps(c)], wr=[dsq])
                kb.op("pe", lambda e, c=c, sq=sq: e.matmul(PS[bk][:, :], lhsT=ONES[:], rhs=sq, start=(c == 0), stop=(c == nch - 1)),
                      rd=[dsq, dC], wr=[dPS[bk]] if c in (0, nch - 1) else [])
            kb.op("act", lambda e: e.activation(out=RSTD[:], in_=PS[bk][:, :], func=AF.Sqrt, bias=EPSB[:, 0:1], scale=1.0 / dtot),
                  rd=[dPS[bk], dC], wr=[dRSTD])
            kb.op("dve", lambda e: e.reciprocal(out=RSTD[:], in_=RSTD[:]), rd=[dRSTD], wr=[dRSTD])

        EPSB = sb("EPSB", [128, 1])

        def norm_mod(gs, sh):
            for t0 in (0, 512):
                rstd_half(lambda c, a, n: X[:, c, a:a + n], lambda c: dX[c], KC, DM, t0)
                for c in range(KC):
                    tmp, dt_ = MS[:, c % 2, :], dMS[c % 2]
                    kb.op("dve", lambda e, c=c, tmp=tmp: e.tensor_tensor(out=tmp, in0=X[:, c, t0:t0 + 512], in1=RSTD[:], op=ALU.mult),
                          rd=[dX[c], dRSTD], wr=[dt_])
                    kb.op("act", lambda e, c=c, tmp=tmp: e.activation(out=Hv[:, c, t0:t0 + 512], in_=tmp, func=AF.Identity,
                                                                      bias=sh[:, c:c + 1], scale=gs[:, c:c + 1]),
                          rd=[dt_, dMOD], wr=[dH(c)])

        def post_norm_residual(ysrc, ydep, gg):
            for t0 in (0, 512):
                rstd_half(ysrc, ydep, KC, DM, t0)
                for c in range(KC):
                    tmp, dt_ = MS[:, c % 2, :], dMS[c % 2]
                    kb.op("dve", lambda e, c=c, tmp=tmp: e.tensor_tensor(out=tmp, in0=ysrc(c, t0, 512), in1=RSTD[:], op=ALU.mult),
                          rd=[ydep(c), dRSTD], wr=[dt_])
                    kb.op("dve", lambda e, c=c, tmp=tmp: e.scalar_tensor_tensor(out=X[:, c, t0:t0 + 512], in0=tmp, scalar=gg[:, c:c + 1],
                                                                                in1=X[:, c, t0:t0 + 512], op0=ALU.mult, op1=ALU.add),
                          rd=[dt_, dMOD], wr=[dX[c]])

        def ymix(c, a, n):
            return (RAf if c < 8 else RCf)[:, c % 8, a:a + n]

        def ymix_dep(c):
            return (dRA if c < 8 else dRC)[c % 8]

        def yffn(c, a, n):
            return (RBf if c < 8 else RCf)[:, c % 8, a:a + n]

        def yffn_dep(c):
            return (dRB if c < 8 else dRC)[c % 8]

        def out_proj(wview, scratch_deps):
            kb.retire(scratch_deps, dRC)

            def cons(oc, t0, tn, bk):
                copy_op(ev_eng(), ymix(oc, t0, tn), PS[bk][:, 0:tn], [dPS[bk]], [ymix_dep(oc)])
            linear_fm(wview, KC, DM, osrc, [dO(c) for c in range(0, KC, 2)], cons)
            post_norm_residual(ymix, ymix_dep, GGM)

        def rope_evac(bk, t0, tn, out_ap, out_deps, np_=128):
            kb.op("act", lambda e: e.activation(out=XBt[0:np_, 0:tn], in_=PS[bk][0:np_, 0:tn], func=AF.Identity, bias=0.0, scale=1.0), rd=[dPS[bk]], wr=[dXB])
            b2 = bank()
            kb.pe_group([lambda e: e.matmul(PS[b2][0:np_, 0:tn], lhsT=PERM[0:np_, 0:np_], rhs=XBt[0:np_, 0:tn], start=True, stop=True)],
                        rd=[dXB, dC], wr=[dPS[b2]])
            t1, t2 = MS[0:np_, 0, 0:tn], MS[0:np_, 1, 0:tn]
            kb.op("dve", lambda e: e.tensor_tensor(out=t1, in0=PS[bk][0:np_, 0:tn], in1=ROPEC[0:np_, t0:t0 + tn], op=ALU.mult),
                  rd=[dPS[bk], dC], wr=[dMS[0]])
            kb.op("dve", lambda e: e.tensor_tensor(out=t2, in0=PS[b2][0:np_, 0:tn], in1=ROPES[0:np_, t0:t0 + tn], op=ALU.mult),
                  rd=[dPS[b2], dC], wr=[dMS[1]])
            kb.op("dve", lambda e: e.tensor_tensor(out=out_ap, in0=t1, in1=t2, op=ALU.add), rd=[dMS[0], dMS[1]], wr=out_deps)

        def rope_evac2(bk, t0, tn, out_lo, out_hi, out_deps):
            if "r3" in skip:
                copy_op(ev_eng(), out_lo, PS[bk][0:64, 0:tn], [dPS[bk]], out_deps)
                return
            if "r4" in skip:
                kb.op("act", lambda e: e.activation(out=XBt[:, 0:tn], in_=PS[bk][:, 0:tn], func=AF.Identity, bias=0.0, scale=1.0), rd=[dPS[bk]], wr=[dXB])
                return
            if "r6" in skip:
                kb.op("act", lambda e: e.activation(out=MS[:, 2, 0:tn], in_=PS[bk][:, 0:tn], func=AF.Identity, bias=0.0, scale=1.0), rd=[dPS[bk]], wr=[dMS[2]])
                return
            if "r7" in skip:
                kb.op("act", lambda e: e.activation(out=XBt[:, 0:tn], in_=PS[bk][:, 0:tn], func=AF.Identity, bias=0.0, scale=1.0), rd=[dPS[bk]], wr=[dXB])
                return
            if "r8" in skip:
                kb.op("dve", lambda e: e.tensor_copy(out=XBt[:, 0:tn], in_=PS[bk][:, 0:tn]), rd=[dPS[bk]], wr=[dXB])
                return
            if "r5" in skip:
                kb.op("dve", lambda e: e.tensor_tensor(out=MS[:, 0, 0:tn], in0=PS[bk][:, 0:tn], in1=ROPEC[:, t0:t0 + tn], op=ALU.mult),
                      rd=[dPS[bk], dC], wr=[dMS[0]])
                return
            kb.op("act", lambda e: e.activation(out=XBt[:, 0:tn], in_=PS[bk][:, 0:tn], func=AF.Identity, bias=0.0, scale=1.0), rd=[dPS[bk]], wr=[dXB])
            b2 = bank()
            if "r2" not in skip:
                kb.pe_group([lambda e: e.matmul(PS[b2][:, 0:tn], lhsT=PERM[:, :], rhs=XBt[:, 0:tn], start=True, stop=True)],
                            rd=[dXB, dC], wr=[dPS[b2]])
            else:
                b2 = bk
            t1, t2 = MS[:, 0, 0:tn], MS[:, 1, 0:tn]
            kb.op("dve", lambda e: e.tensor_tensor(out=t1, in0=PS[bk][:, 0:tn], in1=ROPEC[:, t0:t0 + tn], op=ALU.mult),
                  rd=[dPS[bk], dC] + ([dXB] if "r11" in skip else []), wr=[dMS[0]])
            if "r10" not in skip:
                kb.op("dve", lambda e: e.tensor_tensor(out=t2, in0=PS[b2][:, 0:tn], in1=ROPES[:, t0:t0 + tn], op=ALU.mult),
                      rd=[dPS[b2], dC], wr=[dMS[1]])
            else:
                t2 = t1
            if "r9" in skip:
                return
            kb.op("dve", lambda e: e.tensor_tensor(out=out_lo, in0=t1[0:64, :], in1=t2[0:64, :], op=ALU.add), rd=[dMS[0], dMS[1]], wr=out_deps)
            if "r1" not in skip:
                kb.op("dve", lambda e: e.tensor_tensor(out=out_hi, in0=t1[64:128, :], in1=t2[64:128, :], op=ALU.add), rd=[dMS[0], dMS[1]], wr=out_deps)

        def stage_out(bk, rows, tn, dram_ap, idx):
            stg, dstg = MS[:, 2 + idx % 2, :], dMS[2 + idx % 2]
            copy_op(ev_eng(), stg[0:rows, 0:tn], PS[bk][0:rows, 0:tn], [dPS[bk]], [dstg])
            kb.dma("sp", dram_ap, stg[0:rows, 0:tn], rd=[dstg])

        def exp_tile(sbk, ebuf, edep, scale, n=512):
            kb.op("act", lambda e: e.activation(out=ebuf, in_=PS[sbk][:, 0:n], func=AF.Exp, scale=scale), rd=[dPS[sbk]], wr=[edep])

        def mixer_a(layer_idx):
            lam_init = 0.8 - 0.6 * math.exp(-0.3 * layer_idx)
            LAMT = MS[:, 0, 0:256]
            kb.dma("sp", LAMT, I["a_lam"], wr=[dMS[0]])
            kb.dma("sp", SMALL[:, 16:17], I["a_subln"], wr=[dSMALL])
            pr = MS[:, 1, 0:128]
            kb.op("dve", lambda e: e.tensor_tensor(out=pr[:, 0:64], in0=LAMT[:, 0:64], in1=LAMT[:, 64:128], op=ALU.mult), rd=[dMS[0]], wr=[dMS[1]])
            kb.op("dve", lambda e: e.tensor_tensor(out=pr[:, 64:128], in0=LAMT[:, 128:192], in1=LAMT[:, 192:256], op=ALU.mult), rd=[dMS[0]], wr=[dMS[1]])
            kb.op("dve", lambda e: e.tensor_reduce(out=SMALL[:, 17:18], in_=pr[:, 0:64], axis=AX.X, op=ALU.add), rd=[dMS[1]], wr=[dSMALL])
            kb.op("dve", lambda e: e.tensor_reduce(out=SMALL[:, 18:19], in_=pr[:, 64:128], axis=AX.X, op=ALU.add), rd=[dMS[1]], wr=[dSMALL])
            kb.op("act", lambda e: e.activation(out=SMALL[:, 17:19], in_=SMALL[:, 17:19], func=AF.Exp), rd=[dSMALL], wr=[dSMALL])
            kb.op("dve", lambda e: e.tensor_tensor(out=SMALL[:, 19:20], in0=SMALL[:, 18:19], in1=SMALL[:, 17:18], op=ALU.subtract), rd=[dSMALL], wr=[dSMALL])
            kb.op("dve", lambda e: e.tensor_scalar_add(out=SMALL[:, 19:20], in0=SMALL[:, 19:20], scalar1=-lam_init), rd=[dSMALL], wr=[dSMALL])
            kb.op("dve", lambda e: e.tensor_scalar_mul(out=SMALL[:, 20:21], in0=SMALL[:, 16:17], scalar1=(1.0 - lam_init)), rd=[dSMALL], wr=[dSMALL])
            NEGLAM, SUBG = SMALL[:, 19:20], SMALL[:, 20:21]

            QT = RCb[:, 0:4096].rearrange("p (j t) -> p j t", j=4)
            KT = RCb[:, 4096:9216].rearrange("p (j t) -> p j t", j=4)
            VT = RCb[:, 9216:11776].rearrange("p (k c) -> p k c", k=10)
            Eb = [RCb[:, 11776 + i * 512:11776 + (i + 1) * 512] for i in range(4)]
            OH = RC[:, 6912:6912 + 512]
            dQ, dK, dV, dOH = Dep(), Dep(), Dep(), Dep()
            dE = [Dep() for _ in range(4)]
            sdeps = [dQ, dK, dV, dOH] + dE
            kb.retire(dRC, sdeps)
            scale = 64 ** -0.5
            pending = [None]

            def flush_pending():
                if pending[0] is not None:
                    f = pending[0]
                    pending[0] = None
                    f()
            for hp in range(8):
                def cq(oc, t0, tn, bk):
                    o_ = oc - 2 * hp
                    rope_evac2(bk, t0, tn, QT[0:64, 2 * o_, t0:t0 + tn], QT[0:64, 2 * o_ + 1, t0:t0 + tn], [dQ])
                if "a_q" not in skip:
                    linear_fm(I["a_wq"][:, hp * 256:(hp + 1) * 256], KC, 256, hsrc, [dH(c) for c in range(0, KC, 2)],
                              lambda oc, t0, tn, bk: cq(oc + 2 * hp, t0, tn, bk))

                def ck(oc, t0, tn, bk):
                    h = 2 * hp + oc
                    stage_out(bk, 128, tn, OUT["a_kT"][h * 128:(h + 1) * 128, t0:t0 + tn], oc)
                    rope_evac2(bk, t0, tn, KT[0:64, 2 * oc, NPAST + t0:NPAST + t0 + tn], KT[0:64, 2 * oc + 1, NPAST + t0:NPAST + t0 + tn], [dK])
                if "a_k" not in skip:
                    linear_fm(I["a_wk"][:, hp * 256:(hp + 1) * 256], KC, 256, hsrc, [dH(c) for c in range(0, KC, 2)], ck)
                for j in (range(2) if "a_past" not in skip else ()):
                    h = 2 * hp + j
                    for m in range(2):
                        kb.dma("pool", KT[0:64, 2 * j + m, 0:NPAST], I["pa_kT"][h * 128 + m * 64:h * 128 + (m + 1) * 64, :], wr=[dK])

                def cv(tt, c0, cn, bk):
                    stage_out(bk, 128, 256, OUT["a_v"][tt * 128:(tt + 1) * 128, hp * 256:(hp + 1) * 256], tt)
                    copy_op("act", VT[:, 2 + tt, :], PS[bk][:, 0:256], [dPS[bk]], [dV])
                if "a_v" not in skip:
                    linear_tm(I["a_wv"][:, hp * 256:(hp + 1) * 256], KC, 256, hsrc, [dH(c) for c in range(0, KC, 2)], cv)
                if "a_past" not in skip:
                  kb.dma("pool", VT[:, 0:2, :], I["pa_v"][:, hp * 256:(hp + 1) * 256].rearrange("(k p) c -> p k c", p=128), wr=[dV])
                for j in (range(2) if "a_attn" not in skip else ()):
                    h = 2 * hp + j
                    for t0 in (0, 512):
                        def emit_S(kt):
                            for m in range(2):
                                sbk = m * 2 + (kt % 2)
                                kb.pe_group([
                                    lambda e, m=m, sbk=sbk: e.matmul(PS[sbk][:, :], lhsT=KT[0:64, 2 * j + m, kt * 128:(kt + 1) * 128],
                                                                     rhs=QT[0:64, 2 * j + m, t0:t0 + 512], start=True, stop=True)],
                                    rd=[dQ, dK], wr=[dPS[sbk]])
                                for hh in range(2):
                                    bc = kt * 4 + t0 // 256 + hh
                                    kb.op("act", lambda e, sbk=sbk, hh=hh, bc=bc: e.activation(
                                        out=Eb[sbk][:, hh * 256:(hh + 1) * 256], in_=PS[sbk][:, hh * 256:(hh + 1) * 256], func=AF.Exp,
                                        scale=scale, bias=ABIAS[:, bc:bc + 1]), rd=[dPS[sbk], dC], wr=[dE[sbk]])

                        def emit_PV(kt):
                            for m in range(2):
                                ei = m * 2 + (kt % 2)
                                ob, sbn = 4 + m, 6 + m
                                fl = kt in (0, 9)
                                kb.op("pe", lambda e, ei=ei, ob=ob: e.matmul(PS[ob][:, :], lhsT=VT[:, kt, j * 128:(j + 1) * 128], rhs=Eb[ei],
                                                                             start=(kt == 0), stop=(kt == 9)),
                                      rd=[dV, dE[ei]], wr=[dPS[ob]] if fl else [])
                                kb.op("pe", lambda e, ei=ei, sbn=sbn: e.matmul(PS[sbn][:, :], lhsT=ONES[:], rhs=Eb[ei], start=(kt == 0), stop=(kt == 9)),
                                      rd=[dE[ei], dC], wr=[dPS[sbn]] if fl else [])

                        emit_S(0)
                        for kt in range(10):
                            if kt + 1 < 10:
                                emit_S(kt + 1)
                            if kt == 0:
                                flush_pending()
                            emit_PV(kt)
                        r1, t1, r2, t2 = MS[:, 0, :], MS[:, 1, :], MS[:, 2, :], MS[:, 3, :]
                        kb.op("dve", lambda e: e.reciprocal(out=r1, in_=PS[6][:, :]), rd=[dPS[6]], wr=[dMS[0]])
                        kb.op("dve", lambda e: e.tensor_tensor(out=t1, in0=PS[4][:, :], in1=r1, op=ALU.mult), rd=[dPS[4], dMS[0]], wr=[dMS[1]])
                        kb.op("dve", lambda e: e.reciprocal(out=r2, in_=PS[7][:, :]), rd=[dPS[7]], wr=[dMS[2]])
                        kb.op("dve", lambda e: e.tensor_tensor(out=t2, in0=PS[5][:, :], in1=r2, op=ALU.mult), rd=[dPS[5], dMS[2]], wr=[dMS[3]])
                        kb.op("dve", lambda e: e.scalar_tensor_tensor(out=OH, in0=t2, scalar=NEGLAM, in1=t1, op0=ALU.mult, op1=ALU.add),
                              rd=[dMS[1], dMS[3], dSMALL], wr=[dOH])
                        sq = MS[:, 0, :].bitcast(BF16)[:, 0:512]
                        kb.op("act", lambda e: e.activation(out=sq, in_=OH, func=AF.Square), rd=[dOH], wr=[dMS[0]])

                        def part2(h=h, t0=t0, sq=sq, r2=r2):
                            kb.pe_group([lambda e: e.matmul(PS[6][:, :], lhsT=ONES[:], rhs=sq, start=True, stop=True)], rd=[dMS[0], dC], wr=[dPS[6]])
                            kb.op("act", lambda e: e.activation(out=r2, in_=PS[6][:, :], func=AF.Sqrt, bias=EPSB[:, 0:1], scale=1.0 / 128), rd=[dPS[6], dC], wr=[dMS[2]])
                            kb.op("dve", lambda e: e.reciprocal(out=r2, in_=r2), rd=[dMS[2]], wr=[dMS[2]])
                            kb.op("dve", lambda e: e.scalar_tensor_tensor(out=Ov[:, h, t0:t0 + 512], in0=OH, scalar=SUBG, in1=r2, op0=ALU.mult, op1=ALU.mult),
                                  rd=[dOH, dMS[2], dSMALL], wr=[dO(h)])
                        pending[0] = part2
                flush_pending()
            if "a_out" not in skip:
                out_proj(I["a_wo"], sdeps)

        def mixer_c():
            kb.dma("sp", SMALL[:, 24:28], I["c_qnormT"], wr=[dSMALL])
            kb.dma("sp", SMALL[:, 28:30], I["c_kvnormT"], wr=[dSMALL])
            QLN = RCb[:, 0:4096].rearrange("p (c t) -> p c t", c=4)
            CKVN = RCb[:, 4096:6656].rearrange("p (c t) -> p c t", c=2)
            KPE = RCb[:, 6656:7936]
            QN = RCb[:, 7936:9984].rearrange("p (j t) -> p j t", j=2)
            QR = RCb[:, 9984:12032].rearrange("p (j t) -> p j t", j=2)
            KN = RCb[:, 12032:13312]
            VT = RCb[:, 13312:14592].rearrange("p (k c) -> p k c", k=10)
            Eb = [RCb[:, 14592 + i * 512:14592 + (i + 1) * 512] for i in range(2)]
            QL = RC[:, 4096:8192].rearrange("p (c t) -> p c t", c=4)
            dQLN, dCKVN, dKPE, dQN, dQR, dKN, dV, dQL = [Dep() for _ in range(8)]
            dE = [Dep(), Dep()]
            sdeps = [dQLN, dCKVN, dKPE, dQN, dQR, dKN, dV, dQL] + dE
            kb.retire(dRC, sdeps)
            hd = [dH(c) for c in range(0, KC, 2)]
            linear_fm(I["c_wdq"], KC, 512, hsrc, hd, lambda oc, t0, tn, bk: copy_op(ev_eng(), QL[:, oc, t0:t0 + tn], PS[bk][:, 0:tn], [dPS[bk]], [dQL]))
            for t0 in (0, 512):
                rstd_half(lambda c, a, n: QL[:, c, a:a + n], lambda c: dQL, 4, 512, t0)
                for c in range(4):
                    tmp, dt_ = MS[:, c % 2, :], dMS[c % 2]
                    kb.op("dve", lambda e, c=c, tmp=tmp: e.tensor_tensor(out=tmp, in0=QL[:, c, t0:t0 + 512], in1=RSTD[:], op=ALU.mult), rd=[dQL, dRSTD], wr=[dt_])
                    kb.op("act", lambda e, c=c, tmp=tmp: e.activation(out=QLN[:, c, t0:t0 + 512], in_=tmp, func=AF.Identity, bias=0.0, scale=SMALL[:, 24 + c:25 + c]),
                          rd=[dt_, dSMALL], wr=[dQLN])
            CKV = RC[:, 4096:6144].rearrange("p (c t) -> p c t", c=2)
            dCKV = dQL

            def ckv_cons(oc, t0, tn, bk):
                if oc < 2:
                    copy_op(ev_eng(), CKV[:, oc, t0:t0 + tn], PS[bk][:, 0:tn], [dPS[bk]], [dCKV])
                else:
                    stage_out(bk, 64, tn, OUT["c_kpeT"][:, t0:t0 + tn], t0 // 512)
                    rope_evac(bk, t0, tn, KPE[0:64, NPAST + t0:NPAST + t0 + tn], [dKPE], np_=64)
            linear_fm(I["c_wdkv"], KC, 384, hsrc, hd, ckv_cons)
            kb.dma("pool", KPE[0:64, 0:NPAST], I["pc_kpeT2"][0:64, :], wr=[dKPE])
            for c in range(2):
                kb.dma("pool", CKVN[:, c, 0:NPAST], I["pc_ckvT"][c * 128:(c + 1) * 128, :], wr=[dCKVN])
            for t0 in (0, 512):
                rstd_half(lambda c, a, n: CKV[:, c, a:a + n], lambda c: dCKV, 2, 256, t0)
                for c in range(2):
                    tmp, dt_ = MS[:, c % 2, :], dMS[c % 2]
                    kb.op("dve", lambda e, c=c, tmp=tmp: e.tensor_tensor(out=tmp, in0=CKV[:, c, t0:t0 + 512], in1=RSTD[:], op=ALU.mult), rd=[dCKV, dRSTD], wr=[dt_])
                    stg, dstg = MS[:, 2 + c % 2, :], dMS[2 + c % 2]
                    kb.op("act", lambda e, c=c, tmp=tmp, stg=stg: e.activation(out=stg, in_=tmp, func=AF.Identity, bias=0.0, scale=SMALL[:, 28 + c:29 + c]),
                          rd=[dt_, dSMALL], wr=[dstg])
                    kb.dma("sp", OUT["c_ckvT"][c * 128:(c + 1) * 128, t0:t0 + 512], stg, rd=[dstg])
                    kb.op("dve", lambda e, c=c, stg=stg: e.tensor_copy(out=CKVN[:, c, NPAST + t0:NPAST + t0 + 512], in_=stg), rd=[dstg], wr=[dCKVN])
            kb.retire([dQL], [dQN, dQR, dKN, dV] + dE)
            scale = 192 ** -0.5
            qsrc = lambda k, t0, tn: QLN[:, k, t0:t0 + tn]
            csrc = lambda k, t0, tn: CKVN[:, k, t0:t0 + tn]
            for hp in range(8):
                linear_fm(I["c_wuq"][:, hp * 256:(hp + 1) * 256], 4, 256, qsrc, [dQLN],
                          lambda oc, t0, tn, bk: copy_op(ev_eng(), QN[:, oc, t0:t0 + tn], PS[bk][:, 0:tn], [dPS[bk]], [dQN]))
                linear_fm(I["c_wuq"][:, 2048 + hp * 128:2048 + (hp + 1) * 128], 4, 128, qsrc, [dQLN],
                          lambda oc, t0, tn, bk: rope_evac2(bk, t0, tn, QR[0:64, 0, t0:t0 + tn], QR[0:64, 1, t0:t0 + tn], [dQR]))
                tk, dtk, wk_id = ws.next(I["c_wuk"][:, hp * 256:(hp + 1) * 256], 2, 256)
                tv, dtv, wv_id = ws.next(I["c_wuv"][:, hp * 256:(hp + 1) * 256], 2, 256)
                for j in range(2):
                    h = 2 * hp + j
                    for t0, tn in ((0, 512), (512, 512), (1024, 256)):
                        bk = bank()
                        kb.pe_group([(lambda e, k=k, bk=bk, t0=t0, tn=tn: e.matmul(PS[bk][:, 0:tn], lhsT=tk[:, k, j * 128:(j + 1) * 128], rhs=CKVN[:, k, t0:t0 + tn],
                                                                                   start=(k == 0), stop=(k == 1))) for k in range(2)],
                                    rd=[dtk, dCKVN], wr=[dPS[bk]])
                        copy_op(ev_eng(), KN[:, t0:t0 + tn], PS[bk][:, 0:tn], [dPS[bk]], [dKN])
                    for kt in range(10):
                        bk = bank()
                        kb.pe_group([(lambda e, k=k, bk=bk, kt=kt: e.matmul(PS[bk][:, 0:128], lhsT=CKVN[:, k, kt * 128:(kt + 1) * 128], rhs=tv[:, k, j * 128:(j + 1) * 128],
                                                                             start=(k == 0), stop=(k == 1))) for k in range(2)],
                                    rd=[dtv, dCKVN], wr=[dPS[bk]])
                        copy_op(ev_eng(), VT[:, kt, :], PS[bk][:, 0:128], [dPS[bk]], [dV])
                    for t0 in (0, 512):
                        def emit_S(kt):
                            sbk = kt % 2
                            kb.pe_group([
                                lambda e, sbk=sbk: e.matmul(PS[sbk][:, :], lhsT=KN[:, kt * 128:(kt + 1) * 128], rhs=QN[:, j, t0:t0 + 512], start=True, stop=False),
                                lambda e, sbk=sbk: e.matmul(PS[sbk][:, :], lhsT=KPE[0:64, kt * 128:(kt + 1) * 128],
                                                            rhs=QR[0:64, j, t0:t0 + 512], start=False, stop=True)],
                                rd=[dKN, dQN, dKPE, dQR], wr=[dPS[sbk]])
                            for hh in range(2):
                                bc = kt * 4 + t0 // 256 + hh
                                kb.op("act", lambda e, sbk=sbk, hh=hh, bc=bc: e.activation(
                                    out=Eb[kt % 2][:, hh * 256:(hh + 1) * 256], in_=PS[sbk][:, hh * 256:(hh + 1) * 256], func=AF.Exp,
                                    scale=scale, bias=ABIAS[:, bc:bc + 1]), rd=[dPS[sbk], dC], wr=[dE[kt % 2]])

                        def emit_PV(kt):
                            fl = kt in (0, 9)
                            kb.op("pe", lambda e: e.matmul(PS[4][:, :], lhsT=VT[:, kt, :], rhs=Eb[kt % 2], start=(kt == 0), stop=(kt == 9)),
                                  rd=[dV, dE[kt % 2]], wr=[dPS[4]] if fl else [])
                            kb.op("pe", lambda e: e.matmul(PS[6][:, :], lhsT=ONES[:], rhs=Eb[kt % 2], start=(kt == 0), stop=(kt == 9)),
                                  rd=[dE[kt % 2], dC], wr=[dPS[6]] if fl else [])
                        emit_S(0)
                        for kt in range(10):
                            if kt + 1 < 10:
                                emit_S(kt + 1)
                            emit_PV(kt)
                        r1 = MS[:, 0, :]
                        kb.op("dve", lambda e: e.reciprocal(out=r1, in_=PS[6][:, :]), rd=[dPS[6]], wr=[dMS[0]])
                        kb.op("dve", lambda e: e.tensor_tensor(out=Ov[:, h, t0:t0 + 512], in0=PS[4][:, :], in1=r1, op=ALU.mult), rd=[dPS[4], dMS[0]], wr=[dO(h)])
                ws.release(wk_id)
                ws.release(wv_id)
            out_proj(I["c_wo"], sdeps)

        def mixer_d():
            KD = RCb[:, 0:5120].rearrange("p (g t) -> p g t", g=4)
            VT = RCb[:, 5120:7680].rearrange("p (k c) -> p k c", k=10)
            QD = RCb[:, 7680:15872].rearrange("p (c t) -> p c t", c=8)
            Eb = [MS[:, 2 + i, :].bitcast(BF16)[:, 0:512] for i in range(2)]
            dKD, dV, dQD = Dep(), Dep(), Dep()
            dE = [dMS[2], dMS[3]]
            sdeps = [dKD, dV, dQD] + dE
            kb.retire(dRC, sdeps)
            hd = [dH(c) for c in range(0, KC, 2)]
            kb.dma("pool", TRI[:], I["tri"], wr=[dTRI])

            def ck(oc, t0, tn, bk):
                stage_out(bk, 64, tn, OUT["d_kT"][oc * 64:(oc + 1) * 64, t0:t0 + tn], oc)
                rope_evac(bk, t0, tn, KD[0:64, oc, NPAST + t0:NPAST + t0 + tn], [dKD], np_=64)
            linear_fm(I["d_wk"], KC, 512, hsrc, hd, ck)
            for g in range(4):
                kb.dma("pool", KD[0:64, g, 0:NPAST], I["pd_kT2"][g * 128:g * 128 + 64, :], wr=[dKD])

            def cv(tt, c0, cn, bk):
                stage_out(bk, 128, 256, OUT["d_v"][tt * 128:(tt + 1) * 128, :], tt)
                copy_op("act", VT[:, 2 + tt, :], PS[bk][:, 0:256], [dPS[bk]], [dV])
            linear_tm(I["d_wv"], KC, 256, hsrc, hd, cv)
            kb.dma("pool", VT[:, 0:2, :], I["pd_v"].rearrange("(k p) c -> p k c", p=128), wr=[dV])
            scale = 64 ** -0.5
            for g in range(4):
                linear_fm(I["d_wq"][:, g * 512:(g + 1) * 512], KC, 512, hsrc, hd,
                          lambda oc, t0, tn, bk: rope_evac2(bk, t0, tn, QD[0:64, 2 * oc, t0:t0 + tn], QD[0:64, 2 * oc + 1, t0:t0 + tn], [dQD]))
                for qb in range(8):
                    q0 = qb * 128
                    kts = [(0, None), (1, None)] + [(2 + kb_, kb_ - qb) for kb_ in (qb - 1, qb, qb + 1) if 0 <= kb_ < 8]
                    for par in range(2):
                        p0 = par * 64
                        nkt = len(kts)

                        def emit_S(i):
                            kt, off = kts[i]
                            sbk = i % 2
                            fns = [lambda e: e.matmul(PS[sbk][:, :].rearrange("p (j q) -> p j q", j=4), lhsT=KD[0:64, g, kt * 128:(kt + 1) * 128],
                                                      rhs=QD[0:64, par * 4:(par + 1) * 4, q0:q0 + 128], start=True, stop=(off not in (-1, 1)))]
                            if off in (-1, 1):
                                tsel = 0 if off == -1 else 1
                                fns.append(lambda e: e.matmul(PS[sbk][:, :], lhsT=IDENT[:], rhs=TRI[:, tsel * 512:(tsel + 1) * 512], start=False, stop=True))
                            kb.pe_group(fns, rd=[dKD, dQD, dC, dTRI], wr=[dPS[sbk]])
                            bcol = qb * 5 + i
                            kb.op("act", lambda e: e.activation(out=Eb[i % 2], in_=PS[sbk][:, :], func=AF.Exp, scale=scale, bias=DBIAS[:, bcol:bcol + 1]),
                                  rd=[dPS[sbk], dC], wr=[dE[i % 2]])

                        def emit_PV(i):
                            kt, off = kts[i]
                            fl = i in (0, nkt - 1)
                            kb.op("pe", lambda e: e.matmul(PS[4][0:64, :], lhsT=VT[:, kt, g * 64:(g + 1) * 64], rhs=Eb[i % 2], start=(i == 0), stop=(i == nkt - 1)),
                                  rd=[dV, dE[i % 2]], wr=[dPS[4]] if fl else [])
                            kb.op("pe", lambda e: e.matmul(PS[6][0:64, :], lhsT=ONES[:, 0:64], rhs=Eb[i % 2], start=(i == 0), stop=(i == nkt - 1)),
                                  rd=[dE[i % 2], dC], wr=[dPS[6]] if fl else [])
                        emit_S(0)
                        for i in range(nkt):
                            if i + 1 < nkt:
                                emit_S(i + 1)
                            emit_PV(i)
                        den = MS[0:64, 0, :]
                        es = ESINK[0:64, g * 8 + par * 4:g * 8 + par * 4 + 4].unsqueeze(2).to_broadcast([64, 4, 128])
                        kb.op("dve", lambda e: e.tensor_tensor(out=den.rearrange("p (j q) -> p j q", j=4), in0=PS[6][0:64, :].rearrange("p (j q) -> p j q", j=4),
                                                               in1=es, op=ALU.add), rd=[dPS[6], dC], wr=[dMS[0]])
                        kb.op("dve", lambda e: e.reciprocal(out=den, in_=den), rd=[dMS[0]], wr=[dMS[0]])
                        for e2 in range(2):
                            kb.op("dve", lambda e, e2=e2: e.tensor_tensor(
                                out=Ov[e2 * 64:(e2 + 1) * 64, g * 4 + 2 * par:g * 4 + 2 * par + 2, q0:q0 + 128],
                                in0=PS[4][0:64, :].rearrange("p (i two q) -> p i two q", two=2, q=128)[:, :, e2, :],
                                in1=den.rearrange("p (i two q) -> p i two q", two=2, q=128)[:, :, e2, :], op=ALU.mult),
                                rd=[dPS[4], dMS[0]], wr=[dO(g * 4 + 2 * par)])
            out_proj(I["d_wo"], sdeps)

        def mixer_b(layer_idx):
            kb.op("act", lambda e: e.activation(out=LBT[:], in_=LBT[:], func=AF.Exp), rd=[dC], wr=[dC])
            tot, lbv, oml = SMALL[:, 32:64], EBLB[:, 0:32], EBLB[:, 32:64]
            kb.op("dve", lambda e: e.tensor_tensor(out=tot, in0=LBT[:, 0:32], in1=LBT[:, 32:64], op=ALU.add), rd=[dC], wr=[dSMALL])
            kb.op("dve", lambda e: e.tensor_tensor(out=tot, in0=tot, in1=LBT[:, 64:96], op=ALU.add), rd=[dC, dSMALL], wr=[dSMALL])
            kb.op("dve", lambda e: e.tensor_tensor(out=tot, in0=tot, in1=LBT[:, 96:128], op=ALU.add), rd=[dC, dSMALL], wr=[dSMALL])
            kb.op("dve", lambda e: e.reciprocal(out=tot, in_=tot), rd=[dSMALL], wr=[dSMALL])
            kb.op("dve", lambda e: e.tensor_copy(out=lbv, in_=LBT[:, 32:64]), rd=[dC], wr=[dSMALL])
            for l in range(2, layer_idx + 1):
                kb.op("dve", lambda e, l=l: e.tensor_tensor(out=lbv, in0=lbv, in1=LBT[:, l * 32:(l + 1) * 32], op=ALU.add), rd=[dC, dSMALL], wr=[dSMALL])
            kb.op("dve", lambda e: e.tensor_tensor(out=lbv, in0=lbv, in1=tot, op=ALU.mult), rd=[dSMALL], wr=[dSMALL])
            kb.op("dve", lambda e: e.tensor_scalar(out=oml, in0=lbv, scalar1=-1.0, scalar2=1.0, op0=ALU.mult, op1=ALU.add), rd=[dSMALL], wr=[dSMALL])
            kb.dma("sp", SMALL[:, 21:22], I["b_gnorm"], wr=[dSMALL])
            kb.dma("sp", SMALL[:, 22:23], I["keep"], wr=[dSMALL])
            GN, KEEP = SMALL[:, 21:22], SMALL[:, 22:23]
            kb.dma("pool", SCANM[:], I["scanm"], wr=[dTRI])
            W1, W2, W3, OACC = RC[:, 0:1024], RC[:, 1024:2048], RC[:, 2048:3072], RC[:, 3072:4096]
            QS, SG = RCb[:, 8192:9216], RCb[:, 9216:10240]
            VT = RCb[:, 10240:11264].rearrange("p (k c) -> p k c", k=8)
            QDEC, KINV, KENDT = RCb[:, 11264:12288], RCb[:, 12288:13312], RCb[:, 13312:14336]
            KEND = RCb[:, 14336:15360].rearrange("p (k c) -> p k c", k=8)
            ATM = [RCb[:, 15360 + i * 128:15360 + (i + 1) * 128] for i in range(2)]
            dW1, dW2, dW3, dOACC, dQS, dSG, dV, dQDEC, dKINV, dKENDT, dKEND = [Dep() for _ in range(11)]
            dATM = [Dep(), Dep()]
            sdeps = [dW1, dW2, dW3, dOACC, dQS, dSG, dV, dQDEC, dKINV, dKENDT, dKEND] + dATM
            kb.retire(dRC, sdeps)
            hd = [dH(c) for c in range(0, KC, 2)]
            c3 = lambda ap: ap.rearrange("p (c s) -> p c s", s=32)
            for h in range(16):
                cs = slice(h * 128, (h + 1) * 128)
                linear_fm(I["b_wq"][:, cs], KC, 128, hsrc, hd, blk=128,
                          consume=lambda oc, t0, tn, bk: kb.op("act", lambda e: e.activation(out=QS[:, t0:t0 + tn], in_=PS[bk][:, 0:tn], func=AF.Silu), rd=[dPS[bk]], wr=[dQS]))
                linear_fm(I["b_wg"][:, cs], KC, 128, hsrc, hd, blk=128,
                          consume=lambda oc, t0, tn, bk: kb.op("act", lambda e: e.activation(out=SG[:, t0:t0 + tn], in_=PS[bk][:, 0:tn], func=AF.Silu), rd=[dPS[bk]], wr=[dSG]))
                linear_fm(I["b_wi"][:, cs], KC, 128, hsrc, hd, blk=128,
                          consume=lambda oc, t0, tn, bk: copy_op(ev_eng(), KENDT[:, t0:t0 + tn], PS[bk][:, 0:tn], [dPS[bk]], [dKENDT]))
                for b in range(8):
                    bk = bank()
                    pst = PS[bk][:].bitcast(BF16)
                    kb.pe_group([lambda e, b=b, pst=pst: e.transpose(out=pst[:, 0:128], in_=KENDT[:, b * 128:(b + 1) * 128], identity=IDENT[:])],
                                rd=[dKENDT, dC], wr=[dPS[bk]])
                    copy_op(ev_eng(), VT[:, b, :], pst[:, 0:128], [dPS[bk]], [dV])
                for d in range(2):
                    col = d * 16 + h
                    linear_fm(I["b_wf%d" % d][:, cs], KC, 128, hsrc, hd, blk=128,
                              consume=lambda oc, t0, tn, bk: kb.op("act", lambda e: e.activation(out=W1[:, t0:t0 + tn], in_=PS[bk][:, 0:tn], func=AF.Sigmoid), rd=[dPS[bk]], wr=[dW1]))
                    kb.op("dve", lambda e: e.tensor_scalar(out=W1, in0=W1, scalar1=oml[:, col:col + 1], scalar2=lbv[:, col:col + 1], op0=ALU.mult, op1=ALU.add),
                          rd=[dW1, dSMALL], wr=[dW1])
                    kb.op("act", lambda e: e.activation(out=W2, in_=W1, func=AF.Ln), rd=[dW1], wr=[dW2])
                    kb.op("dve", lambda e: e.tensor_scalar(out=W1, in0=W1, scalar1=-1.0, scalar2=1.0, op0=ALU.mult, op1=ALU.add), rd=[dW1, dW2], wr=[dW1])
                    kb.op("dve", lambda e: e.tensor_tensor_scan(out=W3, data0=SCANM[:], data1=W2, initial=0.0, op0=ALU.mult, op1=ALU.add), rd=[dW2, dTRI], wr=[dW3])
                    kb.op("act", lambda e: e.activation(out=EBL[:], in_=c3(W3)[:, :, 31], func=AF.Exp), rd=[dW3], wr=[dEBL])
                    if d == 0:
                        BC, FREE, dBC, dFREE = W3, W2, dW3, dW2
                    else:
                        kb.op("dve", lambda e: e.tensor_tensor(out=W2, in0=W2, in1=W3, op=ALU.subtract), rd=[dW2, dW3], wr=[dW2])
                        kb.op("dve", lambda e: e.tensor_tensor(out=c3(W2), in0=c3(W2), in1=c3(W3)[:, :, 31:32].to_broadcast([128, 32, 32]), op=ALU.add),
                              rd=[dW2, dW3], wr=[dW2])
                        BC, FREE, dBC, dFREE = W2, W3, dW2, dW3
                    kb.op("act", lambda e: e.activation(out=FREE, in_=BC, func=AF.Exp), rd=[dBC, dEBL], wr=[dFREE])
                    kb.op("dve", lambda e: e.tensor_tensor(out=QDEC, in0=QS, in1=FREE, op=ALU.mult), rd=[dQS, dFREE], wr=[dQDEC])
                    kb.op("act", lambda e: e.activation(out=FREE, in_=BC, func=AF.Exp, scale=-1.0), rd=[dBC, dQDEC], wr=[dFREE])
                    kb.op("dve", lambda e: e.tensor_tensor(out=FREE, in0=FREE, in1=W1, op=ALU.mult), rd=[dFREE, dW1], wr=[dFREE])
                    kb.op("act", lambda e: e.activation(out=KINV, in_=FREE, func=AF.Identity, bias=0.0, scale=1.0), rd=[dFREE], wr=[dKINV])
                    kb.op("dve", lambda e: e.tensor_tensor(out=c3(KENDT), in0=c3(FREE), in1=EBL[:].unsqueeze(2).to_broadcast([128, 32, 32]), op=ALU.mult),
                          rd=[dFREE, dEBL], wr=[dKENDT])
                    for b in range(8):
                        bk = bank()
                        pst = PS[bk][:].bitcast(BF16)
                        kb.pe_group([lambda e, b=b, pst=pst: e.transpose(out=pst[:, 0:128], in_=KENDT[:, b * 128:(b + 1) * 128], identity=IDENT[:])],
                                    rd=[dKENDT, dC], wr=[dPS[bk]])
                        copy_op(ev_eng(), KEND[:, b, :], pst[:, 0:128], [dPS[bk]], [dKEND])
                    kb.dma("sp", SST[:], I["sb_f" if d == 0 else "sb_b"][h], wr=[dSST])
                    SB = [SSTb, SSTb2]
                    dSB = [dSSTb, dSSTb2]
                    cur = [0]
                    kb.op("act", lambda e: e.activation(out=SB[0][:], in_=SST[:], func=AF.Identity, bias=0.0, scale=1.0), rd=[dSST], wr=[dSB[0]])
                    OUTS = OUT["b_f" if d == 0 else "b_b"]
                    blocks = range(8) if d == 0 else range(7, -1, -1)
                    for b in blocks:
                        ab = bank()
                        kb.pe_group([lambda e, b=b, ab=ab: e.matmul(PS[ab][:, 0:128], lhsT=KINV[:, b * 128:(b + 1) * 128], rhs=QDEC[:, b * 128:(b + 1) * 128], start=True, stop=True)],
                                    rd=[dKINV, dQDEC], wr=[dPS[ab]])
                        atm, datm = ATM[b % 2], dATM[b % 2]
                        kb.op("dve", lambda e, ab=ab, atm=atm: e.tensor_tensor(out=atm, in0=PS[ab][:, 0:128], in1=BMASK[:, d * 128:(d + 1) * 128], op=ALU.mult),
                              rd=[dPS[ab], dC], wr=[datm])
                        ob = bank()
                        kb.op("pe", lambda e, b=b, ob=ob, atm=atm: e.matmul(PS[ob][:, 0:128], lhsT=VT[:, b, :], rhs=atm, start=True, stop=False),
                              rd=[dV, datm], wr=[dPS[ob]])
                        ccs = list(range(4)) if d == 0 else list(range(3, -1, -1))
                        ubs = {}
                        for cc in ccs:
                            ub = bank()
                            while ub == ob or ub in ubs.values():
                                ub = bank()
                            ubs[cc] = ub
                            kb.pe_group([lambda e, b=b, cc=cc, ub=ub: e.matmul(PS[ub][:, 0:128], lhsT=KEND[cc * 32:(cc + 1) * 32, b, :], rhs=VT[cc * 32:(cc + 1) * 32, b, :],
                                                                               start=True, stop=True, tile_position=(cc * 32, 0))],
                                        rd=[dKEND, dV], wr=[dPS[ub]])
                        for ci, cc in enumerate(ccs):
                            c = 4 * b + cc
                            boundary = (d == 0 and c % 8 == 0 and c > 0) or (d == 1 and c % 8 == 7 and c < 31)
                            cu = cur[0]
                            if boundary:
                                seq = (c // 8 - 1) if d == 0 else (c + 1) // 8
                                kb.dma("sp", OUTS[seq, h], SST[:], rd=[dSST])
                                kb.op("dve", lambda e, cu=cu: e.tensor_scalar_mul(out=SB[cu][:], in0=SST[:], scalar1=KEEP), rd=[dSST, dSMALL], wr=[dSB[cu]])
                                kb.op("dve", lambda e: e.tensor_scalar_mul(out=SST[:], in0=SST[:], scalar1=KEEP), rd=[dSMALL], wr=[dSST])
                            kb.op("pe", lambda e, c=c, cc=cc, ob=ob, ci=ci, cu=cu: e.matmul(PS[ob][:, cc * 32:(cc + 1) * 32], lhsT=SB[cu][:], rhs=QDEC[:, c * 32:(c + 1) * 32],
                                                                                             start=False, stop=(ci == 3), skip_group_check=True),
                                  rd=[dSB[cu], dQDEC], wr=[dPS[ob]] if ci == 3 else [])
                            ub = ubs[cc]
                            nx = 1 - cu
                            kb.op("dve", lambda e, c=c, ub=ub, nx=nx: e.scalar_tensor_tensor(out=SB[nx][:], in0=SST[:], scalar=EBL[:, c:c + 1], in1=PS[ub][:, 0:128], op0=ALU.mult, op1=ALU.add),
                                  rd=[dSST, dEBL, dPS[ub]], wr=[dSB[nx]])
                            kb.op("dve", lambda e, c=c, ub=ub: e.scalar_tensor_tensor(out=SST[:], in0=SST[:], scalar=EBL[:, c:c + 1], in1=PS[ub][:, 0:128], op0=ALU.mult, op1=ALU.add),
                                  rd=[dEBL, dPS[ub]], wr=[dSST])
                            cur[0] = nx
                        if d == 0:
                            copy_op("act", OACC[:, b * 128:(b + 1) * 128], PS[ob][:, 0:128], [dPS[ob]], [dOACC])
                        else:
                            kb.op("dve", lambda e, b=b, ob=ob: e.tensor_tensor(out=OACC[:, b * 128:(b + 1) * 128], in0=OACC[:, b * 128:(b + 1) * 128], in1=PS[ob][:, 0:128], op=ALU.add),
                                  rd=[dPS[ob]], wr=[dOACC])
                    kb.dma("sp", OUTS[3 if d == 0 else 0, h], SST[:], rd=[dSST])
                for t0 in (0, 512):
                    rstd_half(lambda c, a, n: OACC[:, a:a + n], lambda c: dOACC, 1, 128, t0)
                    tmp = MS[:, 0, :]
                    kb.op("dve", lambda e: e.tensor_tensor(out=tmp, in0=OACC[:, t0:t0 + 512], in1=RSTD[:], op=ALU.mult), rd=[dOACC, dRSTD], wr=[dMS[0]])
                    kb.op("dve", lambda e: e.scalar_tensor_tensor(out=Ov[:, h, t0:t0 + 512], in0=tmp, scalar=GN, in1=SG[:, t0:t0 + 512], op0=ALU.mult, op1=ALU.mult),
                          rd=[dMS[0], dSG, dSMALL], wr=[dO(h)])
            out_proj(I["b_wo"], sdeps)

        EBLB = sb("EBLB", [128, 64])

        def ffn(l):
            norm_mod(GSF, SHF)
            HID = [MS[:, 0:2, :].rearrange("p a b -> p (a b)").bitcast(BF16).rearrange("p (j t) -> p j t", j=2),
                   MS[:, 2:4, :].rearrange("p a b -> p (a b)").bitcast(BF16).rearrange("p (j t) -> p j t", j=2)]
            dHID = [[dMS[0], dMS[1]], [dMS[2], dMS[3]]]
            SGv = [RSTD[:].bitcast(BF16)[:, 0:512], RSTD[:].bitcast(BF16)[:, 512:1024], TRI[:, 0:512], TRI[:, 512:1024]]
            dSG = [dRSTD, dRSTD, dTRI, dTRI]
            hd = [dH(c) for c in range(0, KC, 2)]
            nsl = DFF // 256
            for s in range(nsl):
                hid, dhid = HID[s % 2], dHID[s % 2]
                tg, dtg, wid = ws.next(I["ffn_wg"][l][:, s * 256:(s + 1) * 256], KC, 256)
                for idx, (j, t0) in enumerate(((0, 0), (0, 512), (1, 0), (1, 512))):
                    bg = bank()
                    kb.pe_group([(lambda e, k=k, bg=bg: e.matmul(PS[bg][:, :], lhsT=tg[:, k, j * 128:(j + 1) * 128], rhs=Hv[:, k, t0:t0 + 512],
                                                                  start=(k == 0), stop=(k == KC - 1))) for k in range(KC)], rd=[dtg] + hd, wr=[dPS[bg]])
                    if idx == 3:
                        ws.release(wid)
                    kb.op("act", lambda e, bg=bg, idx=idx: e.activation(out=SGv[idx], in_=PS[bg][:, :], func=AF.Silu), rd=[dPS[bg]], wr=[dSG[idx]])
                tu, dtu, wid = ws.next(I["ffn_wu"][l][:, s * 256:(s + 1) * 256], KC, 256)
                for idx, (j, t0) in enumerate(((0, 0), (0, 512), (1, 0), (1, 512))):
                    bu = bank()
                    kb.pe_group([(lambda e, k=k, bu=bu: e.matmul(PS[bu][:, :], lhsT=tu[:, k, j * 128:(j + 1) * 128], rhs=Hv[:, k, t0:t0 + 512],
                                                                  start=(k == 0), stop=(k == KC - 1))) for k in range(KC)], rd=[dtu] + hd, wr=[dPS[bu]])
                    if idx == 3:
                        ws.release(wid)
                    kb.op("dve", lambda e, bu=bu, j=j, t0=t0, idx=idx: e.tensor_tensor(out=hid[:, j, t0:t0 + 512], in0=PS[bu][:, :], in1=SGv[idx], op=ALU.mult),
                          rd=[dPS[bu], dSG[idx]], wr=dhid)
                td, dtd, wid = ws.next(I["ffn_wd"][l][s * 256:(s + 1) * 256, :], 2, DM)
                for oc in range(KC):
                    for t0 in (0, 512):
                        bk = bank()
                        kb.pe_group([(lambda e, k=k, bk=bk: e.matmul(PS[bk][:, :], lhsT=td[:, k, oc * 128:(oc + 1) * 128], rhs=hid[:, k, t0:t0 + 512],
                                                                      start=(k == 0), stop=(k == 1))) for k in range(2)], rd=[dtd] + dhid, wr=[dPS[bk]])
                        if oc == KC - 1 and t0 == 512:
                            ws.release(wid)
                        if s == 0:
                            copy_op(ev_eng(), yffn(oc, t0, 512), PS[bk][:, :], [dPS[bk]], [yffn_dep(oc)])
                        else:
                            kb.op("dve", lambda e, bk=bk, oc=oc, t0=t0: e.tensor_tensor(out=yffn(oc, t0, 512), in0=yffn(oc, t0, 512), in1=PS[bk][:, :], op=ALU.add),
                                  rd=[dPS[bk]], wr=[yffn_dep(oc)])
            post_norm_residual(yffn, yffn_dep, GGF)

        def emit():
            kb.op("dve", lambda e: e.memset(EPSB[:], EPS), wr=[dC])
            load_consts()
            mark = lambda name: (None if kb.dry else kb.marks.append((name, kb.n_mm)))
            for l in range(n_layers):
                mark("ada%d" % l)
                if "ada" not in skip:
                    ada(l)
                mark("norm%d" % l)
                if "norm" not in skip:
                    norm_mod(GSM, SHM)
                mark("mixer%d" % l)
                m = l % 4
                if ("mixer%d" % m) in skip:
                    pass
                elif m == 0:
                    mixer_a(l)
                elif m == 1:
                    mixer_b(l)
                elif m == 2:
                    mixer_c()
                else:
                    mixer_d()
                mark("ffn%d" % l)
                if "ffn" not in skip:
                    ffn(l)
            mark("end")
            for c in range(KC):
                kb.dma("sp", OUT["yT"][c * 128:(c + 1) * 128, :], X[:, c, :], rd=[dX[c]])
            if not kb.dry:
                for qn in list(kb.dma_ring.keys()):
                    for j, slot in enumerate(kb.dma_ring[qn]):
                        kb._wait(kb.E["sp"], (slot[0], 16 * slot[1], ("dma", qn, j)))

        kb.dry = True
        emit()
        kb.dry = False
        rr[0] = 0
        evq[0] = 0
        emit()
        nc._used_inputs = list(I.keys())
        print("[kernel] marks", kb.marks, flush=True)
        print("[kernel] inst counts", {k: v.n_inst for k, v in kb.E.items()}, "nsem", kb.nsem, "wblocks", len(ws.req), flush=True)
    return nc


def _colT(v):
    v = np.asarray(v, np.float32)
    return np.ascontiguousarray(v.reshape(-1, 128).T)


def _rope_tables(identity):
    C = np.ones((128, T), np.float32)
    S = np.zeros((128, T), np.float32)
    if not identity:
        t = np.arange(T)
        row, col = (t // 64).astype(np.float32), (t % 64).astype(np.float32)
        inv = (np.float32(10000.0) ** (-np.arange(16, dtype=np.float32) / np.float32(16))).astype(np.float32)
        for p in range(128):
            q = p % 64
            pos = row if q < 32 else col
            ang = (pos * inv[q % 16]).astype(np.float32)
            C[p] = np.cos(ang)
            S[p] = -np.sin(ang) if (q % 32) < 16 else np.sin(ang)
    return C, S


def _consts(is_ctx):
    c = {}
    c["ident"] = np.eye(128, dtype=np.float32)
    perm = np.zeros((128, 128), np.float32)
    for m in range(128):
        k = m + 16 if (m % 32) < 16 else m - 16
        perm[k, m] = 1.0
    c["perm"] = perm
    mq = np.zeros((8, T), np.float32)
    mk = np.zeros((8, NK), np.float32)
    keys = np.arange(NK)
    for j in range(4):
        mk[j] = ((keys >= NPAST) & ((keys - NPAST) // 256 == j)).astype(np.float32)
    mk[4] = (keys < NPAST).astype(np.float32)
    tri = np.zeros((128, 2, 4, 128), np.float32)
    if is_ctx:
        q = np.arange(T)
        for j in range(4):
            mq[j] = np.where(q // 256 == j, 0.0, NEGBIG)
        mq[4] = NEGBIG
    else:
        kk = np.arange(128)[:, None]
        qq = np.arange(128)[None, :]
        tri[:, 0] = np.where(kk >= qq, 0.0, NEGBIG)[:, None, :]
        tri[:, 1] = np.where(kk <= qq, 0.0, NEGBIG)[:, None, :]
    c["maskq"], c["maskk"], c["tri"] = mq, mk, tri.reshape(128, 1024)
    db = np.zeros((128, 40), np.float32)
    if is_ctx:
        for qb in range(8):
            kts = [None, None] + [kb_ for kb_ in (qb - 1, qb, qb + 1) if 0 <= kb_ < 8]
            for i, kb_ in enumerate(kts):
                if kb_ is None or (kb_ // 2) != (qb // 2):
                    db[:, qb * 5 + i] = NEGBIG
    c["dbias"] = db
    ab = np.zeros((128, 40), np.float32)
    if is_ctx:
        for kt in range(10):
            for qblk in range(4):
                if kt < 2 or (kt - 2) // 2 != qblk:
                    ab[:, kt * 4 + qblk] = NEGBIG
    c["abias"] = ab
    c["ropeC"], c["ropeS"] = _rope_tables(is_ctx)
    s = np.arange(128)[:, None]
    t = np.arange(128)[None, :]
    same = (s // 32) == (t // 32)
    bm = np.zeros((128, 256), np.float32)
    bm[:, 0:128] = (same & (s <= t)).astype(np.float32)
    bm[:, 128:256] = (same & (s >= t)).astype(np.float32)
    c["bmask"] = bm
    sm = np.ones((128, T), np.float32)
    sm[:, ::32] = 0.0
    c["scanm"] = sm
    return c


_PROGRAM_CACHE = {}
BUILD_ARGS = {}


def kernel(**inp):
    f = lambda k: np.asarray(inp[k], np.float32)
    shared = {}
    shared["ada_w"] = f("ada_w")
    shared["ada_bT"] = np.ascontiguousarray(np.concatenate([_colT(f("ada_b")[l]) for l in range(4)], axis=1))
    shared["gains"] = np.ascontiguousarray(np.concatenate(
        [_colT(f(n)[l]) for n in ("norm_mix_pre", "norm_mix_post", "norm_ffn_pre", "norm_ffn_post") for l in range(4)], axis=1))
    for n in ("a_wq", "a_wk", "a_wv", "a_wo", "b_wq", "b_wi", "b_wg", "b_wo", "c_wdq", "c_wuk", "c_wuv", "c_wo", "d_wq", "d_wv", "d_wo"):
        shared[n] = f(n)[0]
    shared["b_wf0"], shared["b_wf1"] = f("b_wf")[0, 0], f("b_wf")[0, 1]
    shared["a_lam"] = np.ascontiguousarray(np.broadcast_to(f("a_lambda")[0].reshape(1, 256), (128, 256)))
    shared["a_subln"] = f("a_subln")[0].reshape(128, 1)
    shared["b_lowerT"] = np.ascontiguousarray(np.concatenate([_colT(f("b_lower")[l, d]) for l in range(4) for d in range(2)], axis=1))
    shared["b_gnorm"] = f("b_gnorm")[0].reshape(128, 1)
    shared["c_qnormT"] = _colT(f("c_qnorm")[0])
    shared["c_kvnormT"] = _colT(f("c_kvnorm")[0])
    wuq = f("c_wuq")[0]
    pn = [h * 192 + d for h in range(16) for d in range(128)] + [h * 192 + 128 + d for h in range(16) for d in range(64)]
    shared["c_wuq"] = np.ascontiguousarray(wuq[:, pn])
    wdkv = f("c_wdkv")[0]
    shared["c_wdkv"] = np.ascontiguousarray(np.concatenate([wdkv[:, 0:256], wdkv[:, 256:320], wdkv[:, 256:320]], axis=1))
    wk = f("d_wk")[0]
    shared["d_wk"] = np.ascontiguousarray(np.concatenate([wk[:, g * 64:(g + 1) * 64] for g in range(4) for _ in range(2)], axis=1))
    sink = f("d_sink")[0]
    sp = np.asarray(sink, np.float32)
    shared["d_sinkP"] = np.ascontiguousarray(np.broadcast_to(sp.reshape(1, 32), (128, 32)))
    shared["ffn_wg"], shared["ffn_wu"], shared["ffn_wd"] = f("ffn_wg"), f("ffn_wu"), f("ffn_wd")
    cc = {True: _consts(True), False: _consts(False)}

    in_maps = []
    xp, xs = f("x_prompt"), f("x_sample")
    for core in range(8):
        m = dict(shared)
        is_ctx = core < 4
        m.update(cc[is_ctx])
        if is_ctx:
            m["xT"] = np.ascontiguousarray(xp[4 * core:4 * core + 4].reshape(T, DM).T)
            m["condT"] = _colT(f("c_ctx"))
            m["pa_kT"] = np.zeros((DM, NPAST), np.float32)
            m["pa_v"] = np.zeros((NPAST, DM), np.float32)
            m["sb_f"] = np.zeros((16, 128, 128), np.float32)
            m["sb_b"] = np.zeros((16, 128, 128), np.float32)
            m["keep"] = np.zeros((128, 1), np.float32)
            m["pc_ckvT"] = np.zeros((256, NPAST), np.float32)
            m["pc_kpeT2"] = np.zeros((128, NPAST), np.float32)
            m["pd_kT2"] = np.zeros((512, NPAST), np.float32)
            m["pd_v"] = np.zeros((NPAST, 256), np.float32)
        else:
            b = core - 4
            m["xT"] = np.ascontiguousarray(xs[b].T)
            m["condT"] = _colT(f("c")[b])
            m["pa_kT"] = np.ascontiguousarray(f("cache_a_k")[b, 0].reshape(NPAST, DM).T)
            m["pa_v"] = np.ascontiguousarray(f("cache_a_v")[b, 0].reshape(NPAST, DM))
            m["sb_f"] = np.ascontiguousarray(f("state_b_fwd")[b, 0])
            m["sb_b"] = np.ascontiguousarray(f("state_b_bwd")[b, 0])
            m["keep"] = np.ones((128, 1), np.float32)
            m["pc_ckvT"] = np.ascontiguousarray(f("cache_c_ckv")[b, 0].T)
            kpeT = f("cache_c_kpe")[b, 0].T
            m["pc_kpeT2"] = np.ascontiguousarray(np.concatenate([kpeT, kpeT], axis=0))
            dk = f("cache_d_k")[b, 0]
            m["pd_kT2"] = np.ascontiguousarray(np.concatenate([dk[:, g, :].T for g in range(4) for _ in range(2)], axis=0))
            m["pd_v"] = np.ascontiguousarray(f("cache_d_v")[b, 0].reshape(NPAST, 256))
        in_maps.append(m)

    if "nc" not in _PROGRAM_CACHE:
        _PROGRAM_CACHE["nc"] = build_program(**BUILD_ARGS)
    nc = _PROGRAM_CACHE["nc"]
    used = set(nc._used_inputs)
    in_maps = [{k: v for k, v in m.items() if k in used} for m in in_maps]
    res = run_bass_kernel_spmd(nc, in_maps, core_ids=list(range(8)))
    R = res.results

    y_prompt = np.concatenate([R[k]["yT"].T.reshape(4, 256, DM) for k in range(4)], axis=0)
    y_sample = np.stack([R[4 + b]["yT"].T for b in range(4)], axis=0)
    a_k = np.concatenate([R[k]["a_kT"].T.reshape(4, 1, 256, 16, 128) for k in range(4)], axis=0)
    a_v = np.concatenate([R[k]["a_v"].reshape(4, 1, 256, 16, 128) for k in range(4)], axis=0)
    b_f = np.concatenate([R[k]["b_f"].reshape(4, 1, 16, 128, 128) for k in range(4)], axis=0)
    b_b = np.concatenate([R[k]["b_b"].reshape(4, 1, 16, 128, 128) for k in range(4)], axis=0)
    c_ckv = np.concatenate([R[k]["c_ckvT"].T.reshape(4, 1, 256, 256) for k in range(4)], axis=0)
    c_kpe = np.concatenate([R[k]["c_kpeT"].T.reshape(4, 1, 256, 64) for k in range(4)], axis=0)
    d_k = np.concatenate([R[k]["d_kT"].T.reshape(4, 1, 256, 4, 64) for k in range(4)], axis=0)
    d_v = np.concatenate([R[k]["d_v"].reshape(4, 1, 256, 4, 64) for k in range(4)], axis=0)
    outs = (y_prompt, y_sample, a_k, a_v, b_f, b_b, c_ckv, c_kpe, d_k, d_v)
    return tuple(np.ascontiguousarray(o, dtype=np.float32) for o in outs)
```
